# Optimizing a Trainium2 kernel written in Bass

```python
import math
import jax, jax.numpy as jnp
from jax import lax
import numpy as np

D_MODEL = 2048
BATCH = 2
SEQ = 16384
DEPTH = 2

HEAD_DIM = 64
FOX_HEADS = 8
SWA_HEADS = 8
SWA_KV_HEADS = 2
MOBA_HEADS = 8
FOX_WIDTH = FOX_HEADS * HEAD_DIM
SWA_WIDTH = SWA_HEADS * HEAD_DIM
SWA_KV_WIDTH = SWA_KV_HEADS * HEAD_DIM
MOBA_WIDTH = MOBA_HEADS * HEAD_DIM
Q_BLOCK = 128
SWA_WINDOW = 128
MOBA_BLOCK = 256
MOBA_TOPK = 3
MOBA_Q_CHUNK = 64
DEEPNORM_ALPHA = (2.0 * DEPTH) ** 0.25
DEEPNORM_BETA = (8.0 * DEPTH) ** -0.25
LN_EPS = 1e-5
FORGET_BIAS_INIT = 2.0
NEG_INF = -1e30

IN_SEGMENTS = (
    ("fox_q", FOX_WIDTH), ("fox_k", FOX_WIDTH), ("fox_v", FOX_WIDTH), ("fox_z", FOX_WIDTH), ("fox_f", FOX_HEADS),
    ("swa_q", SWA_WIDTH), ("swa_k", SWA_KV_WIDTH), ("swa_v", SWA_KV_WIDTH), ("swa_z", SWA_WIDTH),
    ("moba_q", MOBA_WIDTH), ("moba_k", MOBA_WIDTH), ("moba_v", MOBA_WIDTH), ("moba_z", MOBA_WIDTH),
    ("gate_fox", D_MODEL), ("gate_swa", D_MODEL), ("gate_moba", D_MODEL),
)
N_IN = sum(size for _, size in IN_SEGMENTS)

kernel_name = "hybrid_fox_swa_moba_gated_deepnorm"


def _split_columns(h):
    parts = {}
    start = 0
    for name, size in IN_SEGMENTS:
        parts[name] = h[..., start:start + size]
        start += size
    return parts


def _heads(t, n_heads):
    b, s, _ = t.shape
    return t.reshape(b, s, n_heads, HEAD_DIM).transpose(0, 2, 1, 3)


def _merge_heads(t):
    b, n, s, d = t.shape
    return t.transpose(0, 2, 1, 3).reshape(b, s, n * d)


def _alibi_slopes(n):
    return jnp.power(2.0, -8.0 * jnp.arange(1, n + 1, dtype=jnp.float32) / n)


def layer_norm(x, g, b):
    xf = x.astype(jnp.float32)
    mu = jnp.mean(xf, axis=-1, keepdims=True)
    var = jnp.mean(jnp.square(xf - mu), axis=-1, keepdims=True)
    return ((xf - mu) * lax.rsqrt(var + LN_EPS) * g + b).astype(x.dtype)


def fox_attention(q, k, v, f_logit):
    b, h, s, d = q.shape
    log_f = jax.nn.log_sigmoid(f_logit.astype(jnp.float32))
    c = jnp.cumsum(log_f, axis=1).transpose(0, 2, 1)
    nblk = s // Q_BLOCK
    q_blocks = q.reshape(b, h, nblk, Q_BLOCK, d).transpose(2, 0, 1, 3, 4)
    c_blocks = c.reshape(b, h, nblk, Q_BLOCK).transpose(2, 0, 1, 3)
    key_pos = jnp.arange(s)
    scale = HEAD_DIM ** -0.5

    def one_block(args):
        i, qb, cb = args
        q_pos = i * Q_BLOCK + jnp.arange(Q_BLOCK)
        logits = jnp.einsum('bhqd,bhkd->bhqk', qb, k, preferred_element_type=jnp.float32) * scale
        logits = logits + cb[..., :, None] - c[:, :, None, :]
        causal = key_pos[None, :] <= q_pos[:, None]
        logits = jnp.where(causal, logits, NEG_INF)
        p = jax.nn.softmax(logits, axis=-1).astype(v.dtype)
        return jnp.einsum('bhqk,bhkd->bhqd', p, v)

    out = lax.map(one_block, (jnp.arange(nblk), q_blocks, c_blocks))
    return out.transpose(1, 2, 0, 3, 4).reshape(b, h, s, d)


def swa_attention(q, k, v, sinks, slopes):
    b, hq, s, d = q.shape
    hkv = k.shape[1]
    g = hq // hkv
    nblk = s // Q_BLOCK
    qb = q.reshape(b, hkv, g, nblk, Q_BLOCK, d)
    kb = k.reshape(b, hkv, nblk, Q_BLOCK, d)
    vb = v.reshape(b, hkv, nblk, Q_BLOCK, d)
    k_prev = jnp.concatenate([jnp.zeros_like(kb[:, :, :1]), kb[:, :, :-1]], axis=2)
    v_prev = jnp.concatenate([jnp.zeros_like(vb[:, :, :1]), vb[:, :, :-1]], axis=2)
    k_band = jnp.concatenate([k_prev, kb], axis=3)
    v_band = jnp.concatenate([v_prev, vb], axis=3)
    logits = jnp.einsum('bhgnqd,bhnkd->bhgnqk', qb, k_band,
                        preferred_element_type=jnp.float32) * (HEAD_DIM ** -0.5)
    rel = (Q_BLOCK + jnp.arange(Q_BLOCK))[:, None] - jnp.arange(2 * Q_BLOCK)[None, :]
    in_window = (rel >= 0) & (rel < SWA_WINDOW)
    real_key = (jnp.arange(nblk)[:, None, None] > 0) | (jnp.arange(2 * Q_BLOCK)[None, None, :] >= Q_BLOCK)
    valid = in_window[None] & real_key
    logits = logits - slopes.reshape(hkv, g)[None, :, :, None, None, None] * rel.astype(jnp.float32)
    logits = jnp.where(valid, logits, NEG_INF)
    sink = sinks.astype(jnp.float32).reshape(hkv, g)[None, :, :, None, None, None]
    m = jnp.maximum(jnp.max(logits, axis=-1, keepdims=True), sink)
    e = jnp.exp(logits - m)
    p = (e / (jnp.sum(e, axis=-1, keepdims=True) + jnp.exp(sink - m))).astype(v.dtype)
    out = jnp.einsum('bhgnqk,bhnkd->bhgnqd', p, v_band)
    return out.reshape(b, hq, s, d)


def moba_attention(q, k, v, slopes):
    b, h, s, d = q.shape
    s_pad = -(-s // MOBA_BLOCK) * MOBA_BLOCK
    pad = ((0, 0), (0, 0), (0, s_pad - s), (0, 0))
    q, k, v = jnp.pad(q, pad), jnp.pad(k, pad), jnp.pad(v, pad)
    nb = s_pad // MOBA_BLOCK
    kb = k.reshape(b, h, nb, MOBA_BLOCK, d)
    vb = v.reshape(b, h, nb, MOBA_BLOCK, d)
    k_mean = jnp.mean(kb.astype(jnp.float32), axis=3).astype(k.dtype)
    top = min(MOBA_TOPK, nb)
    nchunk = s_pad // MOBA_Q_CHUNK
    q_chunks = q.reshape(b, h, nchunk, MOBA_Q_CHUNK, d).transpose(2, 0, 1, 3, 4)
    bi = jnp.arange(b)[:, None, None, None]
    hi = jnp.arange(h)[None, :, None, None]
    offs = jnp.arange(MOBA_BLOCK)
    scale = HEAD_DIM ** -0.5

    def one_chunk(args):
        i, qc = args
        q_pos = i * MOBA_Q_CHUNK + jnp.arange(MOBA_Q_CHUNK)
        own = (i * MOBA_Q_CHUNK) // MOBA_BLOCK
        gate = jnp.einsum('bhqd,bhnd->bhqn', qc, k_mean, preferred_element_type=jnp.float32)
        past = jnp.arange(nb)[None, :] < (q_pos // MOBA_BLOCK)[:, None]
        gate = jnp.where(past, gate, NEG_INF)
        _, idx = lax.top_k(gate, top)
        sel_valid = idx < own
        k_sel = kb[bi, hi, idx]
        v_sel = vb[bi, hi, idx]
        l_sel = jnp.einsum('bhqd,bhqnkd->bhqnk', qc, k_sel, preferred_element_type=jnp.float32) * scale
        sel_pos = idx[..., None] * MOBA_BLOCK + offs
        dist_sel = (q_pos[None, None, :, None, None] - sel_pos).astype(jnp.float32)
        l_sel = l_sel - slopes[None, :, None, None, None] * dist_sel
        l_sel = jnp.where(sel_valid[..., None], l_sel, NEG_INF).reshape(b, h, MOBA_Q_CHUNK, top * MOBA_BLOCK)
        k_own = lax.dynamic_slice_in_dim(kb, own, 1, axis=2)[:, :, 0]
        v_own = lax.dynamic_slice_in_dim(vb, own, 1, axis=2)[:, :, 0]
        rel = q_pos[:, None] - (own * MOBA_BLOCK + offs)[None, :]
        l_own = jnp.einsum('bhqd,bhkd->bhqk', qc, k_own, preferred_element_type=jnp.float32) * scale
        l_own = l_own - slopes[None, :, None, None] * rel.astype(jnp.float32)
        l_own = jnp.where(rel >= 0, l_own, NEG_INF)
        p = jax.nn.softmax(jnp.concatenate([l_sel, l_own], axis=-1), axis=-1).astype(v.dtype)
        p_sel = p[..., :top * MOBA_BLOCK].reshape(b, h, MOBA_Q_CHUNK, top, MOBA_BLOCK)
        p_own = p[..., top * MOBA_BLOCK:]
        return (jnp.einsum('bhqnk,bhqnkd->bhqd', p_sel, v_sel)
                + jnp.einsum('bhqk,bhkd->bhqd', p_own, v_own))

    out = lax.map(one_chunk, (jnp.arange(nchunk), q_chunks))
    out = out.transpose(1, 2, 0, 3, 4).reshape(b, h, s_pad, d)
    return out[:, :, :s]


def hybrid_layer(x, w_in, b_in, sinks, w_br_fox, w_br_swa, w_br_moba, w_out, ln_g, ln_b):
    h = jnp.einsum('bsd,dn->bsn', x, w_in) + b_in
    p = _split_columns(h)
    o_fox = fox_attention(_heads(p["fox_q"], FOX_HEADS), _heads(p["fox_k"], FOX_HEADS),
                          _heads(p["fox_v"], FOX_HEADS), p["fox_f"])
    o_swa = swa_attention(_heads(p["swa_q"], SWA_HEADS), _heads(p["swa_k"], SWA_KV_HEADS),
                          _heads(p["swa_v"], SWA_KV_HEADS), sinks, _alibi_slopes(SWA_HEADS))
    o_moba = moba_attention(_heads(p["moba_q"], MOBA_HEADS), _heads(p["moba_k"], MOBA_HEADS),
                            _heads(p["moba_v"], MOBA_HEADS), _alibi_slopes(MOBA_HEADS))
    br_fox = jnp.einsum('bsw,wd->bsd', _merge_heads(o_fox) * jax.nn.silu(p["fox_z"]), w_br_fox)
    br_swa = jnp.einsum('bsw,wd->bsd', _merge_heads(o_swa) * jax.nn.silu(p["swa_z"]), w_br_swa)
    br_moba = jnp.einsum('bsw,wd->bsd', _merge_heads(o_moba) * jax.nn.silu(p["moba_z"]), w_br_moba)
    y = (jax.nn.sigmoid(p["gate_fox"]) * br_fox + jax.nn.sigmoid(p["gate_swa"]) * br_swa
         + jax.nn.sigmoid(p["gate_moba"]) * br_moba)
    out = jnp.einsum('bsd,de->bse', y, w_out)
    return layer_norm(DEEPNORM_ALPHA * x + out, ln_g, ln_b)


def setup_inputs(seed: int = 0) -> dict:
    key = jax.random.key(seed)
    ks = jax.random.split(key, 11)
    f32 = jnp.float32
    x = jax.random.normal(ks[0], (BATCH, SEQ, D_MODEL), f32)
    col_scale = jnp.concatenate([jnp.full((size,), DEEPNORM_BETA if name.endswith("_v") else 1.0, f32)
                                 for name, size in IN_SEGMENTS])
    forget_offset = jnp.concatenate([jnp.full((size,), FORGET_BIAS_INIT if name == "fox_f" else 0.0, f32)
                                     for name, size in IN_SEGMENTS])
    w_in = jax.random.normal(ks[1], (DEPTH, D_MODEL, N_IN), f32) * (D_MODEL ** -0.5) * col_scale
    b_in = 0.02 * jax.random.normal(ks[2], (DEPTH, N_IN), f32) + forget_offset
    swa_sinks = 0.5 * jax.random.normal(ks[3], (DEPTH, SWA_HEADS), f32)
    w_branch_fox = jax.random.normal(ks[4], (DEPTH, FOX_WIDTH, D_MODEL), f32) * (FOX_WIDTH ** -0.5) * DEEPNORM_BETA
    w_branch_swa = jax.random.normal(ks[5], (DEPTH, SWA_WIDTH, D_MODEL), f32) * (SWA_WIDTH ** -0.5) * DEEPNORM_BETA
    w_branch_moba = jax.random.normal(ks[6], (DEPTH, MOBA_WIDTH, D_MODEL), f32) * (MOBA_WIDTH ** -0.5) * DEEPNORM_BETA
    w_out = jax.random.normal(ks[7], (DEPTH, D_MODEL, D_MODEL), f32) * (D_MODEL ** -0.5) * DEEPNORM_BETA
    ln_gain = 1.0 + 0.02 * jax.random.normal(ks[8], (DEPTH, D_MODEL), f32)
    ln_bias = 0.02 * jax.random.normal(ks[9], (DEPTH, D_MODEL), f32)
    return {"x": x, "w_in": w_in, "b_in": b_in, "swa_sinks": swa_sinks,
            "w_branch_fox": w_branch_fox, "w_branch_swa": w_branch_swa, "w_branch_moba": w_branch_moba,
            "w_out": w_out, "ln_gain": ln_gain, "ln_bias": ln_bias}


def reference(x, w_in, b_in, swa_sinks, w_branch_fox, w_branch_swa, w_branch_moba, w_out, ln_gain, ln_bias):
    for l in range(DEPTH):
        x = hybrid_layer(x, w_in[l], b_in[l], swa_sinks[l], w_branch_fox[l], w_branch_swa[l],
                         w_branch_moba[l], w_out[l], ln_gain[l], ln_bias[l])
    return x
```

```python
import contextlib
import numpy as np
import ml_dtypes
import concourse.bass as bass
import concourse.mybir as mybir
from concourse.bass_utils import run_bass_kernel_spmd

F32 = mybir.dt.float32
BF16 = mybir.dt.bfloat16
AF = mybir.ActivationFunctionType
ALU = mybir.AluOpType
AX = mybir.AxisListType

NCORES = 8
D = 2048
KC = D // 128
HD = 64
NH = 8
DEPTH = 2
ALPHA = (2.0 * DEPTH) ** 0.25
LN_EPS = 1e-5
NEGBIG = -30000.0
SEM_ROLL = 30000

SEG = {}
_o = 0
for _n, _s in (("fox_q", 512), ("fox_k", 512), ("fox_v", 512), ("fox_z", 512), ("fox_f", 8),
               ("swa_q", 512), ("swa_k", 128), ("swa_v", 128), ("swa_z", 512),
               ("moba_q", 512), ("moba_k", 512), ("moba_v", 512), ("moba_z", 512),
               ("gate_fox", D), ("gate_swa", D), ("gate_moba", D)):
    SEG[_n] = (_o, _s)
    _o += _s
N_IN = _o


class Buf:
    __slots__ = ("name", "w", "r")

    def __init__(self, name=""):
        self.name = name
        self.w = None
        self.r = {}


class DSem:
    def __init__(self, prog, name):
        self.h = prog.ctx.enter_context(prog.nc.semaphore(name))
        self.count = 0
        self.key = ("d", name)


class Prog:
    def __init__(self, nc, ctx, same_engine_sync=True):
        self.nc = nc
        self.ctx = ctx
        self.same_sync = same_engine_sync
        self.eng = {}
        self.semh = {}
        for nm, h in (("pe", nc.tensor), ("act", nc.scalar), ("dve", nc.vector),
                      ("pool", nc.gpsimd), ("sp", nc.sync)):
            self.eng[nm] = {"h": h, "name": nm, "gen": 0, "count": 0, "waited": {}}
            self._newsem(self.eng[nm])
        self.dsems = []
        self.n_inst = 0

    def _newsem(self, e):
        e["gen"] += 1
        e["count"] = 0
        e["key"] = ("e", e["name"], e["gen"])
        e["sem"] = self.ctx.enter_context(self.nc.semaphore(f"s_{e['name']}{e['gen']}"))
        self.semh[e["key"]] = e["sem"]

    def dsem(self, name=None):
        d = DSem(self, name or f"dq{len(self.dsems)}")
        self.dsems.append(d)
        self.semh[d.key] = d.h
        return d

    def _wait(self, e, key, val):
        if val <= e["waited"].get(key, 0):
            return
        e["waited"][key] = val
        e["h"].wait_ge(self.semh[key], val)
        self.n_inst += 1

    def _deps(self, e, reads, writes, same_ok, skip=None):
        k = e["key"]
        for b in reads:
            if b.w is not None and not (same_ok and b.w[0] == k) and b.w[0] != skip:
                self._wait(e, *b.w)
        for b in writes:
            if b.w is not None and not (same_ok and b.w[0] == k) and b.w[0] != skip:
                self._wait(e, *b.w)
            for rk, rv in b.r.items():
                if not (same_ok and rk == k) and rk != skip:
                    self._wait(e, rk, rv)

    @staticmethod
    def _mark(tok, reads, writes):
        for b in reads:
            if b.r.get(tok[0], 0) < tok[1]:
                b.r[tok[0]] = tok[1]
        for b in writes:
            b.w = tok
            b.r = {}

    def op(self, eng, fn, reads=(), writes=(), same_ok=None):
        e = self.eng[eng]
        if same_ok is None:
            same_ok = (eng == "pe") or not self.same_sync
        if e["count"] >= SEM_ROLL:
            self._newsem(e)
        self._deps(e, reads, writes, same_ok)
        ins = fn()
        if isinstance(ins, (list, tuple)):
            self.n_inst += len(ins)
            ins = ins[-1]
        else:
            self.n_inst += 1
        e["count"] += 1
        ins.then_inc(e["sem"], 1)
        tok = (e["key"], e["count"])
        self._mark(tok, reads, writes)
        return tok

    def dma(self, q, out, in_, dsem, reads=(), writes=(), **kw):
        e = self.eng[q]
        self._deps(e, reads, writes, False, skip=dsem.key)
        ins = e["h"].dma_start(out=out, in_=in_, **kw)
        dsem.count += 16
        ins.then_inc(dsem.h, 16)
        self.n_inst += 1
        tok = (dsem.key, dsem.count)
        self._mark(tok, reads, writes)
        return tok

    def finish(self, eng="sp"):
        e = self.eng[eng]
        for d in self.dsems:
            if d.count:
                self._wait(e, d.key, d.count)


def _sb(nc, ctx, name, shape, dt):
    return ctx.enter_context(nc.sbuf_tensor("sb_" + name, list(shape), dt))


def _ps(nc, ctx, name, shape, dt=F32):
    return ctx.enter_context(nc.psum_tensor("pp_" + name, list(shape), dt))


FM_SEGS = ("fox_q", "fox_k", "swa_q", "moba_q", "moba_k", "swa_k")
TM_SEGS = ("fox_v", "swa_v", "moba_v")
QK_ROW = {}
_o = 0
for _n in FM_SEGS:
    QK_ROW[_n] = _o
    _o += SEG[_n][1]
NQK = _o
V_COL = {}
_o = 0
for _n in TM_SEGS:
    V_COL[_n] = _o
    _o += SEG[_n][1]
NTM = _o
NA = NQK + 8 + NTM


def a_col_order():
    cols = []
    for n in FM_SEGS:
        cols += list(range(SEG[n][0], SEG[n][0] + SEG[n][1]))
    cols += list(range(SEG["fox_f"][0], SEG["fox_f"][0] + 8))
    for n in TM_SEGS:
        cols += list(range(SEG[n][0], SEG[n][0] + SEG[n][1]))
    return np.array(cols)


def build_A(TPC):
    nc = bass.Bass("TRN2", target_bir_lowering=False)
    xT = nc.dram_tensor("xT", [D, TPC], F32, kind="ExternalInput").ap()
    wA = nc.dram_tensor("wA", [D, NA], F32, kind="ExternalInput").ap()
    nblk = (NQK + 127) // 128 + 1
    bfm = nc.dram_tensor("bfm", [128, nblk], F32, kind="ExternalInput").ap()
    btm = nc.dram_tensor("btm", [1, NTM], F32, kind="ExternalInput").ap()
    qkT = nc.dram_tensor("qkT", [NQK, TPC], BF16, kind="ExternalOutput").ap()
    fT = nc.dram_tensor("fT", [8, TPC], F32, kind="ExternalOutput").ap()
    vtm = nc.dram_tensor("vtm", [TPC, NTM], BF16, kind="ExternalOutput").ap()
    HT = min(2048, TPC)
    groups = []
    c = 0
    while c < NQK:
        n = min(512, NQK - c)
        groups.append((c, n, "fm"))
        c += n
    groups.append((NQK, 8, "f"))
    c = NQK + 8
    while c < NA:
        n = min(512, NA - c)
        groups.append((c, n, "tm"))
        c += n
    with contextlib.ExitStack() as ctx:
        P = Prog(nc, ctx)
        xh = _sb(nc, ctx, "xh", [128, KC, HT], BF16)
        b_xh = Buf("xh")
        d_xh = P.dsem("d_xh")
        wb = [_sb(nc, ctx, f"wb{i}", [128, KC, 512], BF16) for i in range(2)]
        b_wb = [Buf(f"wb{i}") for i in range(2)]
        d_wb = [P.dsem(f"d_wb{i}") for i in range(2)]
        bfm_sb = _sb(nc, ctx, "bfm_sb", [128, nblk], F32)
        btm_sb = _sb(nc, ctx, "btm_sb", [128, NTM], F32)
        b_bias = Buf("bias")
        d_bias = P.dsem("d_bias")
        NST = 4
        st = [_sb(nc, ctx, f"st{i}", [128, 512], BF16) for i in range(NST)]
        stf = [_sb(nc, ctx, f"stf{i}", [8, 512], F32) for i in range(2)]
        b_st = [Buf(f"st{i}") for i in range(NST)]
        b_stf = [Buf(f"stf{i}") for i in range(2)]
        d_st = [P.dsem(f"d_st{i}") for i in range(NST)]
        d_stf = [P.dsem(f"d_stf{i}") for i in range(2)]
        NPS = 4
        ps = [_ps(nc, ctx, f"ps{i}", [128, 512]) for i in range(NPS)]
        b_ps = [Buf(f"ps{i}") for i in range(NPS)]
        P.dma("sp", bfm_sb[:], bfm, d_bias, writes=[b_bias])
        P.dma("sp", btm_sb[:], btm.partition_broadcast(128), d_bias, writes=[b_bias])
        ips = 0
        ist = 0
        istf = 0
        gi = 0
        for half in range(TPC // HT):
            t0 = half * HT
            for kc in range(KC):
                P.dma("pool", xh[:, kc, :], xT[kc * 128:(kc + 1) * 128, t0:t0 + HT], d_xh, writes=[b_xh])
            for (c0, ncols, kind) in groups:
                w_i = gi % 2
                gi += 1
                P.dma("pool", wb[w_i][:, :, 0:ncols],
                      wA[:, c0:c0 + ncols].rearrange("(kc p) n -> p kc n", p=128),
                      d_wb[w_i], writes=[b_wb[w_i]])
                for tt in range(HT // 512):
                    tok0 = t0 + tt * 512
                    xs = slice(tt * 512, (tt + 1) * 512)
                    if kind in ("fm", "f"):
                        for b0 in range(0, ncols, 128):
                            bw = min(128, ncols - b0)
                            pb = ips % NPS
                            ips += 1
                            blk = (c0 + b0) // 128

                            def mm(pb=pb, bw=bw, b0=b0, w_i=w_i, xs=xs):
                                return [nc.tensor.matmul(ps[pb][0:bw, :], wb[w_i][:, kc, b0:b0 + bw],
                                                         xh[:, kc, xs], start=(kc == 0), stop=(kc == KC - 1))
                                        for kc in range(KC)]
                            P.op("pe", mm, reads=[b_wb[w_i], b_xh], writes=[b_ps[pb]])
                            if kind == "fm":
                                si = ist % NST
                                ist += 1
                                P.op("act", lambda pb=pb, bw=bw, si=si, blk=blk: nc.scalar.activation(
                                    st[si][0:bw, :], ps[pb][0:bw, :], AF.Identity, bias=bfm_sb[0:bw, blk:blk + 1]),
                                    reads=[b_ps[pb], b_bias], writes=[b_st[si]])
                                P.dma("sp", qkT[c0 + b0:c0 + b0 + bw, tok0:tok0 + 512], st[si][0:bw, :], d_st[si],
                                      reads=[b_st[si]])
                            else:
                                si = istf % 2
                                istf += 1
                                P.op("act", lambda pb=pb, si=si, blk=blk: nc.scalar.activation(
                                    stf[si][:, :], ps[pb][0:8, :], AF.Identity, bias=bfm_sb[0:8, blk:blk + 1]),
                                    reads=[b_ps[pb], b_bias], writes=[b_stf[si]])
                                P.dma("sp", fT[:, tok0:tok0 + 512], stf[si][:, :], d_stf[si], reads=[b_stf[si]])
                    else:
                        vc0 = c0 - NQK - 8
                        for sub in range(4):
                            pb = ips % NPS
                            ips += 1
                            ts_ = slice(tt * 512 + sub * 128, tt * 512 + (sub + 1) * 128)

                            def mm(pb=pb, w_i=w_i, ts_=ts_, ncols=ncols):
                                return [nc.tensor.matmul(ps[pb][:, 0:ncols], xh[:, kc, ts_], wb[w_i][:, kc, 0:ncols],
                                                         start=(kc == 0), stop=(kc == KC - 1)) for kc in range(KC)]
                            P.op("pe", mm, reads=[b_wb[w_i], b_xh], writes=[b_ps[pb]])
                            si = ist % NST
                            ist += 1
                            P.op("dve", lambda pb=pb, si=si, ncols=ncols, vc0=vc0: nc.vector.tensor_tensor(
                                st[si][:, 0:ncols], ps[pb][:, 0:ncols], btm_sb[:, vc0:vc0 + ncols], ALU.add),
                                reads=[b_ps[pb], b_bias], writes=[b_st[si]])
                            P.dma("sp", vtm[tok0 + sub * 128:tok0 + (sub + 1) * 128, vc0:vc0 + ncols],
                                  st[si][:, 0:ncols], d_st[si], reads=[b_st[si]])
        P.finish("sp")
        print("build_A n_inst", P.n_inst)
    return nc


def host_A_inputs(x_l, w_in_l, b_in_l, TPC):
    order = a_col_order()
    wA = np.ascontiguousarray(w_in_l[:, order])
    bA = b_in_l[order]
    nblk = (NQK + 127) // 128 + 1
    bfm = np.zeros((128, nblk), np.float32)
    for blk in range(NQK // 128):
        bfm[:, blk] = bA[blk * 128:(blk + 1) * 128]
    bfm[0:8, NQK // 128] = bA[NQK:NQK + 8]
    btm = np.ascontiguousarray(bA[NQK + 8:][None, :])
    maps = []
    for c in range(NCORES):
        xs = x_l[c * TPC:(c + 1) * TPC]
        maps.append({"xT": np.ascontiguousarray(xs.T), "wA": wA, "bfm": bfm, "btm": btm})
    return maps


PSBIG = 240000.0


def b_consts(S):
    NKC = S // 128
    bf = ml_dtypes.bfloat16
    k_ = np.arange(128)[:, None]
    q_ = np.arange(512)[None, :]
    cmask = np.zeros((128, 4, 512), np.float32)
    for j in range(4):
        cmask[:, j, :] = np.where(q_ >= 128 * j + k_, 0.0, -PSBIG)
    q1 = np.arange(128)[None, :]
    smask = np.zeros((128, 2, 128), np.float32)
    smask[:, 0, :] = np.where(q1 >= k_, 0.0, -PSBIG)
    smask[:, 1, :] = np.where(q1 < k_, 0.0, -PSBIG)
    onehot = (np.arange(S)[None, :] // 256 == np.arange(64)[:, None]).astype(np.float32)
    tri = (np.arange(128)[:, None] < np.arange(128)[None, :]).astype(np.float32)
    shared = {"cmask": cmask.astype(bf), "smask": smask.astype(bf), "onehot": onehot.astype(bf),
              "identb": np.eye(128, dtype=np.float32).astype(bf), "identf": np.eye(128, dtype=np.float32),
              "tri": tri}
    per_core = []
    for c in range(NCORES):
        slope = 2.0 ** (-(c + 1))
        pos = np.arange(S, dtype=np.float64)
        mkb = (slope * pos).reshape(NKC, 128).T.astype(np.float32)
        aq = (-8.0 * slope * pos)
        maq = (aq - PSBIG).reshape(NKC, 128).T.astype(np.float32)
        skb = np.stack([slope * np.arange(128), slope * np.arange(128) - 128.0 * slope], axis=1).astype(np.float32)
        sqrow = (-8.0 * slope * (np.arange(S) % 128)).astype(np.float32)[None, :].astype(bf)
        per_core.append({"mkb": mkb, "maq": maq, "skb": skb, "sqrow": sqrow})
    return shared, per_core


def build_B(S, NBATCH=2, do=("fox", "moba", "swa")):
    NKC = S // 128
    NQB = S // 512
    NB = S // 256
    nc = bass.Bass("TRN2", target_bir_lowering=False)

    def din(name, shape, dt):
        return nc.dram_tensor(name, list(shape), dt, kind="ExternalInput").ap()
    qT = {k: din(k + "_q", [NBATCH, 64, S], BF16) for k in ("fox", "swa", "moba")}
    kT = {k: din(k + "_k", [NBATCH, 64, S], BF16) for k in ("fox", "swa", "moba")}
    vv = {k: din(k + "_v", [NBATCH, S, 64], BF16) for k in ("fox", "swa", "moba")}
    ff = din("fox_f", [NBATCH, S], F32)
    sink = din("sink", [1, 1], F32)
    cmask_d = din("cmask", [128, 4, 512], BF16)
    smask_d = din("smask", [128, 2, 128], BF16)
    onehot_d = din("onehot", [64, S], BF16)
    identb_d = din("identb", [128, 128], BF16)
    identf_d = din("identf", [128, 128], F32)
    tri_d = din("tri", [128, 128], F32)
    mkb_d = din("mkb", [128, NKC], F32)
    maq_d = din("maq", [128, NKC], F32)
    skb_d = din("skb", [128, 2], F32)
    sqrow_d = din("sqrow", [1, S], BF16)
    oT = nc.dram_tensor("oT", [3, NBATCH, 65, S], F32, kind="ExternalOutput").ap()
    scr = nc.dram_tensor("scr_sig", [NBATCH, S], BF16).ap()
    KIDX = {"fox": 0, "swa": 1, "moba": 2}

    with contextlib.ExitStack() as ctx:
        P = Prog(nc, ctx)
        sb = lambda n, s, d: _sb(nc, ctx, n, s, d)
        KT = [sb(f"KT{i}", [128, S], BF16) for i in range(2)]
        QT = [sb(f"QT{i}", [128, S], BF16) for i in range(2)]
        VA = [sb(f"VA{i}", [128, NKC, 65], BF16) for i in range(2)]
        tab = [sb(f"tab{i}", [128, NKC], F32) for i in range(2)]
        b_KT = [Buf() for _ in range(2)]
        b_QTq = [Buf() for _ in range(2)]
        b_QTa = [Buf() for _ in range(2)]
        b_VA = [Buf() for _ in range(2)]
        b_tab = [Buf() for _ in range(2)]
        d_KT = [P.dsem(f"d_KT{i}") for i in range(2)]
        d_QT = [P.dsem(f"d_QT{i}") for i in range(2)]
        d_QTa = [P.dsem(f"d_QTa{i}") for i in range(2)]
        d_VA = [P.dsem(f"d_VA{i}") for i in range(2)]
        cmask = sb("cmask", [128, 4, 512], BF16)
        smask = sb("smask", [128, 2, 128], BF16)
        identb = sb("identb", [128, 128], BF16)
        identf = sb("identf", [128, 128], F32)
        tri = sb("tri", [128, 128], F32)
        onesf = sb("onesf", [128, 128], F32)
        mkb = sb("mkb", [128, NKC], F32)
        maq = sb("maq", [128, NKC], F32)
        skb = sb("skb", [128, 2], F32)
        esink = sb("esink", [128, 1], F32)
        b_const = Buf("const")
        d_const = P.dsem("d_const")
        for dst, src in ((cmask, cmask_d), (smask, smask_d), (identb, identb_d), (identf, identf_d),
                         (tri, tri_d), (mkb, mkb_d), (maq, maq_d), (skb, skb_d)):
            P.dma("sp", dst[:], src, d_const, writes=[b_const])
        P.dma("sp", esink[64:65, :], sink, d_const, writes=[b_const])
        for i in range(2):
            P.dma("sp", KT[i][64:128, :], onehot_d, d_const, writes=[b_const])
        b_ones, b_VA1, b_esink = Buf("ones"), Buf("VA1"), Buf("esink")
        P.op("dve", lambda: nc.vector.memset(onesf[:], 1.0), writes=[b_ones])
        P.op("dve", lambda: [nc.vector.memset(VA[i][:, :, 64:65], 1.0) for i in range(2)], writes=[b_VA1])
        P.op("act", lambda: nc.scalar.activation(esink[64:65, :], esink[64:65, :], AF.Exp),
             reads=[b_const], writes=[b_esink])

        NSB = 4
        psS = [_ps(nc, ctx, f"psS{i}", [128, 512]) for i in range(NSB)]
        b_psS = [Buf(f"psS{i}") for i in range(NSB)]
        psO = [_ps(nc, ctx, f"psO{i}", [128, 512]) for i in range(2)]
        b_psO = [Buf(f"psO{i}") for i in range(2)]
        psM = _ps(nc, ctx, "psM", [128, 512])
        b_psM = Buf("psM")
        psT = _ps(nc, ctx, "psT", [128, 128], BF16)
        b_psT = Buf("psT")
        PT = [sb(f"PT{i}", [128, 512], BF16) for i in range(NSB)]
        b_PT = [Buf(f"PT{i}") for i in range(NSB)]
        NOST = 3
        ost = [sb(f"ost{i}", [65, 512], F32) for i in range(NOST)]
        b_ost = [Buf(f"ost{i}") for i in range(NOST)]
        d_ost = [P.dsem(f"d_ost{i}") for i in range(NOST)]
        fkc = sb("fkc", [128, 128], F32)
        fcs = sb("fcs", [128, 128], F32)
        foff = sb("foff", [128, 1], F32)
        fsig = sb("fsig", [128, 128], BF16)
        b_f = Buf("f")
        d_f = P.dsem("d_f")
        d_scr = P.dsem("d_scr")
        b_scr = Buf("scr")
        km = sb("km", [64, 64], F32)
        kmh = sb("kmh", [64, 64], BF16)
        kmhf = sb("kmhf", [64, 64], F32)
        kml = sb("kml", [64, 64], BF16)
        b_km = Buf("km")
        work = sb("work", [128, 64], F32)
        top8 = sb("top8", [128, 8], F32)
        sel = sb("sel", [128, 64], F32)
        MT = sb("MT", [128, 128], BF16)
        b_work, b_top8, b_sel, b_MT = Buf("work"), Buf("top8"), Buf("sel"), Buf("MT")
        st = {"ips": 0, "io": 0, "iost": 0}

        def pro_loads(kind, b, s):
            ops = []
            ops.append(lambda: P.dma("sp", KT[s][0:64, :], kT[kind][b], d_KT[s], writes=[b_KT[s]]))
            ops.append(lambda: P.dma("sp", QT[s][0:64, :], qT[kind][b], d_QT[s], writes=[b_QTq[s]]))
            vsrc = vv[kind][b].rearrange("(kc p) d -> p kc d", p=128)
            nq = 4 if NKC >= 4 else 1
            for qi in range(nq):
                ks = slice(qi * (NKC // nq), (qi + 1) * (NKC // nq))
                ops.append(lambda ks=ks: P.dma("pool", VA[s][:, ks, 0:64], vsrc[:, ks, :], d_VA[s], writes=[b_VA[s]]))
            return ops

        def pro_fox(b, s):
            dm = pro_loads("fox", b, s)
            dm.append(lambda: P.dma("sp", fkc[0:NKC, :], ff[b].rearrange("(kc p) -> kc p", p=128), d_f, writes=[b_f]))
            cp = []
            cp.append(lambda: P.op("act", lambda: nc.scalar.activation(fkc[0:NKC, :], fkc[0:NKC, :], AF.Exp, scale=-1.0),
                                   reads=[b_f], writes=[b_f]))
            cp.append(lambda: P.op("act", lambda: nc.scalar.activation(fkc[0:NKC, :], fkc[0:NKC, :], AF.Ln, bias=1.0),
                                   reads=[b_f], writes=[b_f]))
            cp.append(lambda: P.op("dve", lambda: nc.vector.tensor_tensor_scan(fcs[0:NKC, :], onesf[0:NKC, :], fkc[0:NKC, :],
                                                                               0.0, ALU.mult, ALU.add),
                                   reads=[b_f, b_ones], writes=[b_f]))
            cp.append(lambda: P.op("pe", lambda: nc.tensor.matmul(psM[0:NKC, 0:1], tri[0:NKC, 0:NKC], fcs[0:NKC, 127:128],
                                                                  start=True, stop=True),
                                   reads=[b_f, b_const], writes=[b_psM]))
            cp.append(lambda: P.op("dve", lambda: nc.vector.tensor_copy(foff[0:NKC, :], psM[0:NKC, 0:1]),
                                   reads=[b_psM], writes=[b_f]))
            cp.append(lambda: P.op("dve", lambda: nc.vector.tensor_scalar(fcs[0:NKC, :], fcs[0:NKC, :], foff[0:NKC, 0:1], None,
                                                                          ALU.add), reads=[b_f], writes=[b_f]))
            cp.append(lambda: P.op("dve", lambda: nc.vector.tensor_scalar(fsig[0:NKC, :], fcs[0:NKC, :], -8.0, None, ALU.mult),
                                   reads=[b_f], writes=[b_f]))
            cp.append(lambda: P.dma("sp", scr[b].rearrange("(kc p) -> kc p", p=128), fsig[0:NKC, :], d_scr,
                                    reads=[b_f], writes=[b_scr]))
            cp.append(lambda: P.dma("sp", QT[s][64:128, :], scr[b].partition_broadcast(64), d_QTa[s],
                                    reads=[b_scr], writes=[b_QTa[s]]))
            cp.append(lambda: P.op("pe", lambda: nc.tensor.transpose(psM[:, 0:NKC], fcs[0:NKC, :], identf[0:NKC, 0:NKC]),
                                   reads=[b_f, b_const], writes=[b_psM]))
            cp.append(lambda: P.op("dve", lambda: nc.vector.tensor_copy(tab[s][:, :], psM[:, 0:NKC]),
                                   reads=[b_psM], writes=[b_tab[s]]))
            return dm, cp

        def pro_swa(b, s):
            dm = pro_loads("swa", b, s)
            dm.append(lambda: P.dma("sp", QT[s][64:128, :], sqrow_d[0].partition_broadcast(64), d_QTa[s],
                                    writes=[b_QTa[s]]))
            return dm, []

        def pro_moba(b, s):
            dm = pro_loads("moba", b, s)
            cp = []
            cp.append(lambda: P.op("dve", lambda: nc.vector.tensor_reduce(
                km[:, 0:NB], KT[s][0:64, :].rearrange("d (n k) -> d n k", k=256), AX.X, ALU.add),
                reads=[b_KT[s]], writes=[b_km]))
            cp.append(lambda: P.op("dve", lambda: nc.vector.tensor_scalar(km[:, 0:NB], km[:, 0:NB], 1.0 / 256.0, None, ALU.mult),
                                   reads=[b_km], writes=[b_km]))
            cp.append(lambda: P.op("dve", lambda: nc.vector.tensor_copy(kmh[:, 0:NB], km[:, 0:NB]), reads=[b_km], writes=[b_km]))
            cp.append(lambda: P.op("dve", lambda: nc.vector.tensor_copy(kmhf[:, 0:NB], kmh[:, 0:NB]), reads=[b_km], writes=[b_km]))
            cp.append(lambda: P.op("dve", lambda: nc.vector.tensor_tensor(kml[:, 0:NB], km[:, 0:NB], kmhf[:, 0:NB], ALU.subtract),
                                   reads=[b_km], writes=[b_km]))
            cp.append(lambda: P.op("dve", lambda: nc.vector.memset(work[:, :], -1e30), writes=[b_work]))
            cp.append(lambda: P.op("dve", lambda: nc.vector.memset(sel[:, :], 0.0), writes=[b_sel]))
            cp.append(lambda: P.op("dve", lambda: nc.vector.memset(MT[:, :], 0.0), writes=[b_MT]))
            cp.append(lambda: P.op("dve", lambda: nc.vector.tensor_copy(tab[s][:, :], mkb[:, :]),
                                   reads=[b_const], writes=[b_tab[s]]))
            for t in range(NKC):
                nb = t // 2
                qs = slice(t * 128, (t + 1) * 128)
                if nb > 0:
                    cp.append(lambda qs=qs: P.op("pe", lambda: [
                        nc.tensor.matmul(psM[:, 0:NB], QT[s][0:64, qs], kmh[:, 0:NB], start=True, stop=False),
                        nc.tensor.matmul(psM[:, 0:NB], QT[s][0:64, qs], kml[:, 0:NB], start=False, stop=True)],
                        reads=[b_QTq[s], b_km], writes=[b_psM]))
                    cp.append(lambda nb=nb: P.op("dve", lambda: nc.vector.tensor_copy(work[:, 0:nb], psM[:, 0:nb]),
                                                 reads=[b_psM], writes=[b_work]))
                    cp.append(lambda: P.op("dve", lambda: nc.vector.max(top8[:, :], work[:, 0:max(NB, 8)]),
                                           reads=[b_work], writes=[b_top8]))
                    cp.append(lambda nb=nb: P.op("dve", lambda: nc.vector.tensor_scalar(
                        sel[:, 0:nb], work[:, 0:nb], top8[:, 2:3], None, ALU.is_ge),
                        reads=[b_work, b_top8], writes=[b_sel]))
                if t % 2 == 0:
                    cp.append(lambda nb=nb: P.op("dve", lambda: nc.vector.memset(sel[:, nb:nb + 1], 1.0), writes=[b_sel]))
                cp.append(lambda t=t: P.op("dve", lambda: nc.vector.tensor_scalar(
                    MT[:, 64:128], sel[:, :], PSBIG, maq[:, t:t + 1], ALU.mult, ALU.add),
                    reads=[b_sel, b_const], writes=[b_MT]))
                cp.append(lambda: P.op("pe", lambda: nc.tensor.transpose(psT[:, :], MT[:, :], identb[:, :]),
                                       reads=[b_MT, b_const], writes=[b_psT]))
                cp.append(lambda qs=qs: P.op("dve", lambda: nc.vector.tensor_copy(QT[s][64:128, qs], psT[64:128, :]),
                                             reads=[b_psT], writes=[b_QTa[s]]))
            return dm, cp

        PRO = {"fox": pro_fox, "swa": pro_swa, "moba": pro_moba}

        class Feeder:
            def __init__(self, dm, cp, delay, rate):
                self.dm, self.cp, self.delay, self.rate, self.n = list(dm), list(cp), delay, rate, 0

            def tick(self):
                if self.n == 0:
                    for f in self.dm:
                        f()
                    self.dm = []
                self.n += 1
                if self.n > self.delay:
                    for _ in range(self.rate):
                        if self.cp:
                            self.cp.pop(0)()

            def flush(self):
                for f in self.dm:
                    f()
                for f in self.cp:
                    f()
                self.dm, self.cp = [], []

        def epilogue(kidx, b, ob, q0, nq, add_sink):
            oi = st["iost"] % NOST
            st["iost"] += 1
            P.op("dve", lambda: nc.vector.tensor_copy(ost[oi][:, 0:nq], psO[ob][0:65, 0:nq]),
                 reads=[b_psO[ob]], writes=[b_ost[oi]])
            if add_sink:
                P.op("dve", lambda: nc.vector.tensor_scalar(ost[oi][64:65, 0:nq], ost[oi][64:65, 0:nq],
                                                            esink[64:65, 0:1], None, ALU.add),
                     reads=[b_ost[oi], b_esink], writes=[b_ost[oi]])
            P.dma("sp", oT[kidx, b, :, q0:q0 + nq], ost[oi][:, 0:nq], d_ost[oi], reads=[b_ost[oi]])

        LOOK = 2

        def dense_pass(kidx, b, s, feeder):
            tiles = [(qb, kc) for qb in range(NQB) for kc in range(4 * (qb + 1))]
            pend = []

            def do_pv():
                pqb, pkc, psi = pend.pop(0)
                ob = pqb % 2
                last = pkc == 4 * (pqb + 1) - 1
                P.op("pe", lambda: nc.tensor.matmul(psO[ob][0:65, :], VA[s][:, pkc, :], PT[psi][:, :],
                                                    start=(pkc == 0), stop=last),
                     reads=[b_VA[s], b_VA1, b_PT[psi]], writes=[b_psO[ob]])
                if last:
                    epilogue(kidx, b, ob, pqb * 512, 512, False)
            for (qb, kc) in tiles:
                si = st["ips"] % NSB
                st["ips"] += 1
                diag = kc >= 4 * qb
                j = kc - 4 * qb

                def qk(si=si, qb=qb, kc=kc, diag=diag, j=j):
                    r = [nc.tensor.matmul(psS[si][:, :], KT[s][:, kc * 128:(kc + 1) * 128],
                                          QT[s][:, qb * 512:(qb + 1) * 512], start=True, stop=not diag)]
                    if diag:
                        r.append(nc.tensor.matmul(psS[si][:, :], identb[:, :], cmask[:, j, :],
                                                  start=False, stop=True))
                    return r
                P.op("pe", qk, reads=[b_KT[s], b_QTq[s], b_QTa[s], b_const], writes=[b_psS[si]])
                P.op("act", lambda si=si, kc=kc: nc.scalar.activation(PT[si][:, :], psS[si][:, :], AF.Exp,
                                                                      bias=tab[s][:, kc:kc + 1], scale=0.125),
                     reads=[b_psS[si], b_tab[s]], writes=[b_PT[si]])
                pend.append((qb, kc, si))
                if len(pend) > LOOK:
                    do_pv()
                if feeder is not None:
                    feeder.tick()
            while pend:
                do_pv()

        def swa_pass(b, s, feeder):
            pend = []

            def do_pv():
                g, sa, sp_ = pend.pop(0)
                ob = st["io"] % 2
                st["io"] += 1

                def pv():
                    r = []
                    for jj in range(4):
                        blk = 4 * g + jj
                        cs = slice(jj * 128, (jj + 1) * 128)
                        if blk > 0:
                            r.append(nc.tensor.matmul(psO[ob][0:65, cs], VA[s][:, blk - 1, :], PT[sp_][:, cs],
                                                      start=True, stop=False))
                        r.append(nc.tensor.matmul(psO[ob][0:65, cs], VA[s][:, blk, :], PT[sa][:, cs],
                                                  start=(blk == 0), stop=True))
                    return r
                P.op("pe", pv, reads=[b_VA[s], b_VA1, b_PT[sa], b_PT[sp_]], writes=[b_psO[ob]])
                epilogue(1, b, ob, g * 512, 512, True)
            for g in range(NQB):
                sa = st["ips"] % NSB
                st["ips"] += 1
                sp_ = st["ips"] % NSB
                st["ips"] += 1

                def qk_own(sa=sa, g=g):
                    r = []
                    for jj in range(4):
                        blk = 4 * g + jj
                        cs = slice(jj * 128, (jj + 1) * 128)
                        r.append(nc.tensor.matmul(psS[sa][:, cs], KT[s][:, blk * 128:(blk + 1) * 128],
                                                  QT[s][:, blk * 128:(blk + 1) * 128], start=True, stop=False))
                        r.append(nc.tensor.matmul(psS[sa][:, cs], identb[:, :], smask[:, 0, :], start=False, stop=True))
                    return r

                def qk_prev(sp_=sp_, g=g):
                    r = []
                    for jj in range(4):
                        blk = 4 * g + jj
                        cs = slice(jj * 128, (jj + 1) * 128)
                        if blk > 0:
                            r.append(nc.tensor.matmul(psS[sp_][:, cs], KT[s][:, (blk - 1) * 128:blk * 128],
                                                      QT[s][:, blk * 128:(blk + 1) * 128], start=True, stop=False))
                        r.append(nc.tensor.matmul(psS[sp_][:, cs], identb[:, :], smask[:, 1, :],
                                                  start=(blk == 0), stop=True))
                    return r
                P.op("pe", qk_own, reads=[b_KT[s], b_QTq[s], b_QTa[s], b_const], writes=[b_psS[sa]])
                P.op("pe", qk_prev, reads=[b_KT[s], b_QTq[s], b_QTa[s], b_const], writes=[b_psS[sp_]])
                P.op("act", lambda sa=sa: nc.scalar.activation(PT[sa][:, :], psS[sa][:, :], AF.Exp,
                                                               bias=skb[:, 0:1], scale=0.125),
                     reads=[b_psS[sa], b_const], writes=[b_PT[sa]])
                P.op("act", lambda sp_=sp_: nc.scalar.activation(PT[sp_][:, :], psS[sp_][:, :], AF.Exp,
                                                                 bias=skb[:, 1:2], scale=0.125),
                     reads=[b_psS[sp_], b_const], writes=[b_PT[sp_]])
                pend.append((g, sa, sp_))
                if len(pend) > 1:
                    do_pv()
                if feeder is not None:
                    feeder.tick()
            while pend:
                do_pv()

        passes = [(k, b) for b in range(NBATCH) for k in do]
        dm0, cp0 = PRO[passes[0][0]](passes[0][1], 0)
        Feeder(dm0, cp0, 0, 1).flush()
        for p, (kind, b) in enumerate(passes):
            s = p % 2
            feeder = None
            if p + 1 < len(passes):
                nk, nb_ = passes[p + 1]
                dm, cp = PRO[nk](nb_, (p + 1) % 2)
                if kind == "swa":
                    feeder = Feeder(dm, cp, 8, 2)
                else:
                    feeder = Feeder(dm, cp, 64, 1)
            if kind == "swa":
                swa_pass(b, s, feeder)
            else:
                dense_pass(KIDX[kind], b, s, feeder)
            if feeder is not None:
                feeder.flush()
        P.finish("sp")
        print("build_B n_inst", P.n_inst)
    return nc


def host_B_inputs(qk_full, f_full, v_full, sinks_l, S, NBATCH=2):
    shared, per_core = b_consts(S)
    maps = []

    def qk(name, h):
        r0 = QK_ROW[name] + 64 * h
        return np.ascontiguousarray(qk_full[r0:r0 + 64].reshape(64, NBATCH, S).transpose(1, 0, 2))

    def vh(name, h):
        c0 = V_COL[name] + 64 * h
        return np.ascontiguousarray(v_full[:, c0:c0 + 64].reshape(NBATCH, S, 64))
    for c in range(NCORES):
        m = dict(shared)
        m.update(per_core[c])
        m["fox_q"], m["fox_k"], m["fox_v"] = qk("fox_q", c), qk("fox_k", c), vh("fox_v", c)
        m["moba_q"], m["moba_k"], m["moba_v"] = qk("moba_q", c), qk("moba_k", c), vh("moba_v", c)
        m["swa_q"], m["swa_k"], m["swa_v"] = qk("swa_q", c), qk("swa_k", c // 4), vh("swa_v", c // 4)
        m["fox_f"] = np.ascontiguousarray(f_full[c].reshape(NBATCH, S))
        m["sink"] = np.ascontiguousarray(sinks_l[c].reshape(1, 1)).astype(np.float32)
        maps.append(m)
    return maps


OC = 256
BR = ("fox", "swa", "moba")


def build_C(TPC):
    nc = bass.Bass("TRN2", target_bir_lowering=False)

    def din(name, shape, dt=F32):
        return nc.dram_tensor(name, list(shape), dt, kind="ExternalInput").ap()
    xT = din("xT", [D, TPC])
    xtm = din("xtm", [TPC, D])
    oT3 = din("oT3", [3, 8 * 65, TPC])
    wZG = din("wZG", [60, 128, KC, 128])
    bZG = din("bZG", [128, 60])
    wBr = din("wBr", [48, 128, 4, 128])
    wOut = din("wOut", [D // OC, 128, KC, OC])
    lng = din("lng", [1, D])
    lnb = din("lnb", [1, D])
    y = nc.dram_tensor("y", [TPC, D], F32, kind="ExternalOutput").ap()
    TT = min(1024, TPC)
    NS = TT // 512
    HS = 512 // 128
    with contextlib.ExitStack() as ctx:
        P = Prog(nc, ctx)
        sb = lambda n, s, d: _sb(nc, ctx, n, s, d)
        xTt = sb("xTt", [128, KC, TT], BF16)
        ozT = sb("ozT", [128, 12, TT], BF16)
        yT = sb("yT", [128, KC, TT], BF16)
        b_xTt, b_yT = Buf("xTt"), Buf("yT")
        b_ozT = [Buf(f"ozT{i}") for i in range(3)]
        d_xTt = P.dsem("d_xTt")
        NW = 4
        wblk = [sb(f"wblk{i}", [128, KC, 128], BF16) for i in range(NW)]
        b_wblk = [Buf(f"wblk{i}") for i in range(NW)]
        d_wblk = [P.dsem(f"d_wblk{i}") for i in range(NW)]
        wbr = [sb(f"wbr{i}", [128, 4, 128], BF16) for i in range(3)]
        b_wbr = [Buf(f"wbr{i}") for i in range(3)]
        d_wbr = [P.dsem(f"d_wbr{i}") for i in range(3)]
        wo = [sb(f"wo{i}", [128, KC, OC], BF16) for i in range(2)]
        b_wo = [Buf(f"wo{i}") for i in range(2)]
        d_wo = [P.dsem(f"d_wo{i}") for i in range(2)]
        bzg = sb("bzg", [128, 60], F32)
        gbc = sb("gbc", [128, D], F32)
        bbc = sb("bbc", [128, D], F32)
        b_const = Buf("const")
        d_const = P.dsem("d_const")
        P.dma("sp", bzg[:], bZG, d_const, writes=[b_const])
        P.dma("sp", gbc[:], lng[0].partition_broadcast(128), d_const, writes=[b_const])
        P.dma("sp", bbc[:], lnb[0].partition_broadcast(128), d_const, writes=[b_const])
        ot = [sb(f"ot{i}", [128, 512], F32) for i in range(2)]
        rd = [sb(f"rd{i}", [128, 512], F32) for i in range(2)]
        b_ot = [Buf(f"ot{i}") for i in range(2)]
        d_ot = [P.dsem(f"d_ot{i}") for i in range(2)]
        sz = [sb(f"sz{i}", [128, 512], F32) for i in range(2)]
        b_sz = [Buf(f"sz{i}") for i in range(2)]
        sg = [sb(f"sg{i}", [128, 512], F32) for i in range(2)]
        b_sg = [Buf(f"sg{i}") for i in range(2)]
        yacc = [sb(f"yacc{i}", [128, 512], F32) for i in range(NS)]
        b_yacc = [Buf(f"yacc{i}") for i in range(NS)]
        tmp = sb("tmp", [128, 512], F32)
        b_tmp = Buf("tmp")
        r = [sb(f"r{i}", [128, D], F32) for i in range(HS)]
        b_r = [Buf(f"r{i}") for i in range(HS)]
        d_rl = [P.dsem(f"d_rl{i}") for i in range(HS)]
        d_rs = [P.dsem(f"d_rs{i}") for i in range(HS)]
        nst = D // 512
        stats = sb("stats", [128, nst, 6], F32)
        mv = sb("mv", [128, 2], F32)
        rstd = sb("rstd", [128, 1], F32)
        b_stats = Buf("stats")
        NPS = 6
        ps = [_ps(nc, ctx, f"ps{i}", [128, 512]) for i in range(NPS)]
        b_ps = [Buf(f"ps{i}") for i in range(NPS)]
        c = {"ps": 0, "w": 0, "ot": 0, "sz": 0, "sg": 0, "wo": 0}

        def nxt(k, n):
            v = c[k] % n
            c[k] += 1
            return v

        def proj16(pb, wi, s):
            return [nc.tensor.matmul(ps[pb][:, :], wblk[wi][:, kc, :], xTt[:, kc, s * 512:(s + 1) * 512],
                                     start=(kc == 0), stop=(kc == KC - 1)) for kc in range(KC)]

        def load_xT(tile):
            tk = tile * TT
            for kc in range(KC):
                P.dma("pool", xTt[:, kc, :], xT[kc * 128:(kc + 1) * 128, tk:tk + TT], d_xTt, writes=[b_xTt])

        load_xT(0)
        for tile in range(TPC // TT):
            tok0 = tile * TT
            for br in range(3):
                for wc in range(4):
                    zi = br * 4 + wc
                    wi = nxt("w", NW)
                    P.dma("pool", wblk[wi][:], wZG[zi], d_wblk[wi], writes=[b_wblk[wi]])
                    for s in range(NS):
                        pb = nxt("ps", NPS)
                        P.op("pe", lambda pb=pb, wi=wi, s=s: proj16(pb, wi, s),
                             reads=[b_wblk[wi], b_xTt], writes=[b_ps[pb]])
                        oi = nxt("ot", 2)
                        tsl = slice(tok0 + s * 512, tok0 + (s + 1) * 512)
                        for hh in range(2):
                            r0 = (2 * wc + hh) * 65
                            P.dma("sp", ot[oi][hh * 64:(hh + 1) * 64, :], oT3[br, r0:r0 + 64, tsl], d_ot[oi],
                                  writes=[b_ot[oi]])
                            P.dma("sp", rd[oi][hh * 64:(hh + 1) * 64, :], oT3[br, r0 + 64, tsl].partition_broadcast(64),
                                  d_ot[oi], writes=[b_ot[oi]])
                        P.op("dve", lambda oi=oi: nc.vector.reciprocal(rd[oi][:], rd[oi][:]),
                             reads=[b_ot[oi]], writes=[b_ot[oi]])
                        P.op("dve", lambda oi=oi: nc.vector.tensor_tensor(ot[oi][:], ot[oi][:], rd[oi][:], ALU.mult),
                             reads=[b_ot[oi]], writes=[b_ot[oi]])
                        zi_ = nxt("sz", 2)
                        P.op("act", lambda pb=pb, zi_=zi_, zi=zi: nc.scalar.activation(
                            sz[zi_][:], ps[pb][:], AF.Silu, bias=bzg[:, zi:zi + 1]),
                            reads=[b_ps[pb], b_const], writes=[b_sz[zi_]])
                        P.op("dve", lambda zi_=zi_, oi=oi, zi=zi, s=s: nc.vector.tensor_tensor(
                            ozT[:, zi, s * 512:(s + 1) * 512], sz[zi_][:], ot[oi][:], ALU.mult),
                            reads=[b_sz[zi_], b_ot[oi]], writes=[b_ozT[br]])
            for dc in range(KC):
                for br in range(3):
                    gi = 12 + dc * 3 + br
                    wi = nxt("w", NW)
                    P.dma("pool", wblk[wi][:], wZG[gi], d_wblk[wi], writes=[b_wblk[wi]])
                    P.dma("pool", wbr[br][:], wBr[dc * 3 + br], d_wbr[br], writes=[b_wbr[br]])
                    for s in range(NS):
                        pg = nxt("ps", NPS)
                        P.op("pe", lambda pg=pg, wi=wi, s=s: proj16(pg, wi, s),
                             reads=[b_wblk[wi], b_xTt], writes=[b_ps[pg]])
                        gi_ = nxt("sg", 2)
                        P.op("act", lambda pg=pg, gi_=gi_, gi=gi: nc.scalar.activation(
                            sg[gi_][:], ps[pg][:], AF.Sigmoid, bias=bzg[:, gi:gi + 1]),
                            reads=[b_ps[pg], b_const], writes=[b_sg[gi_]])
                        pbr = nxt("ps", NPS)
                        P.op("pe", lambda pbr=pbr, br=br, s=s: [
                            nc.tensor.matmul(ps[pbr][:, :], wbr[br][:, wc, :], ozT[:, br * 4 + wc, s * 512:(s + 1) * 512],
                                             start=(wc == 0), stop=(wc == 3)) for wc in range(4)],
                            reads=[b_wbr[br], b_ozT[br]], writes=[b_ps[pbr]])
                        ysl = yT[:, dc, s * 512:(s + 1) * 512]
                        if br == 0:
                            P.op("dve", lambda pbr=pbr, gi_=gi_, s=s: nc.vector.tensor_tensor(
                                yacc[s][:], ps[pbr][:], sg[gi_][:], ALU.mult),
                                reads=[b_ps[pbr], b_sg[gi_]], writes=[b_yacc[s]])
                        else:
                            P.op("dve", lambda pbr=pbr, gi_=gi_: nc.vector.tensor_tensor(
                                tmp[:], ps[pbr][:], sg[gi_][:], ALU.mult),
                                reads=[b_ps[pbr], b_sg[gi_]], writes=[b_tmp])
                            if br == 1:
                                P.op("dve", lambda s=s: nc.vector.tensor_tensor(yacc[s][:], yacc[s][:], tmp[:], ALU.add),
                                     reads=[b_yacc[s], b_tmp], writes=[b_yacc[s]])
                            else:
                                P.op("dve", lambda s=s, ysl=ysl: nc.vector.tensor_tensor(ysl, yacc[s][:], tmp[:], ALU.add),
                                     reads=[b_yacc[s], b_tmp], writes=[b_yT])
            if tile + 1 < TPC // TT:
                load_xT(tile + 1)
            for hf in range(NS):
                for sub in range(HS):
                    t0 = tok0 + hf * 512 + sub * 128
                    P.dma("sp", r[sub][:], xtm[t0:t0 + 128, :], d_rl[sub], writes=[b_r[sub]])
                for ob in range(D // OC):
                    wi = nxt("wo", 2)
                    P.dma("pool", wo[wi][:], wOut[ob], d_wo[wi], writes=[b_wo[wi]])
                    for sub in range(HS):
                        pb = nxt("ps", NPS)
                        tsl = slice(hf * 512 + sub * 128, hf * 512 + (sub + 1) * 128)
                        P.op("pe", lambda pb=pb, wi=wi, tsl=tsl: [
                            nc.tensor.matmul(ps[pb][:, 0:OC], yT[:, kc, tsl], wo[wi][:, kc, :],
                                             start=(kc == 0), stop=(kc == KC - 1)) for kc in range(KC)],
                            reads=[b_yT, b_wo[wi]], writes=[b_ps[pb]])
                        P.op("dve", lambda pb=pb, sub=sub, ob=ob: nc.vector.scalar_tensor_tensor(
                            r[sub][:, ob * OC:(ob + 1) * OC], r[sub][:, ob * OC:(ob + 1) * OC], float(ALPHA),
                            ps[pb][:, 0:OC], ALU.mult, ALU.add),
                            reads=[b_ps[pb], b_r[sub]], writes=[b_r[sub]])
                for sub in range(HS):
                    t0 = tok0 + hf * 512 + sub * 128
                    P.op("dve", lambda sub=sub: [nc.vector.bn_stats(stats[:, i, :], r[sub][:, i * 512:(i + 1) * 512])
                                                 for i in range(nst)],
                         reads=[b_r[sub]], writes=[b_stats])
                    P.op("dve", lambda: nc.vector.bn_aggr(mv[:, :], stats[:].rearrange("p a b -> p (a b)")),
                         reads=[b_stats], writes=[b_stats])
                    P.op("act", lambda: nc.scalar.activation(rstd[:, :], mv[:, 1:2], AF.Sqrt, bias=float(LN_EPS)),
                         reads=[b_stats], writes=[b_stats])
                    P.op("dve", lambda: nc.vector.reciprocal(rstd[:, :], rstd[:, :]), reads=[b_stats], writes=[b_stats])
                    P.op("dve", lambda sub=sub: nc.vector.tensor_scalar(r[sub][:], r[sub][:], mv[:, 0:1], rstd[:, 0:1],
                                                                        ALU.subtract, ALU.mult),
                         reads=[b_r[sub], b_stats], writes=[b_r[sub]])
                    P.op("dve", lambda sub=sub: nc.vector.tensor_tensor(r[sub][:], r[sub][:], gbc[:], ALU.mult),
                         reads=[b_r[sub], b_const], writes=[b_r[sub]])
                    P.op("dve", lambda sub=sub: nc.vector.tensor_tensor(r[sub][:], r[sub][:], bbc[:], ALU.add),
                         reads=[b_r[sub], b_const], writes=[b_r[sub]])
                    P.dma("sp", y[t0:t0 + 128, :], r[sub][:], d_rs[sub], reads=[b_r[sub]])
        P.finish("sp")
        print("build_C n_inst", P.n_inst)
    return nc


def host_C_weights(w_in_l, b_in_l, wbr_l, wout_l, lng_l, lnb_l):
    blocks = []
    bias = np.zeros((128, 60), np.float32)
    bi = 0
    for br in BR:
        c0 = SEG[br + "_z"][0]
        for wc in range(4):
            blocks.append((c0 + wc * 128))
    for dc in range(KC):
        for br in BR:
            blocks.append(SEG["gate_" + br][0] + dc * 128)
    wZG = np.empty((60, 128, KC, 128), np.float32)
    for bi, c0 in enumerate(blocks):
        wZG[bi] = w_in_l[:, c0:c0 + 128].reshape(KC, 128, 128).transpose(1, 0, 2)
        bias[:, bi] = b_in_l[c0:c0 + 128]
    wBr = np.empty((48, 128, 4, 128), np.float32)
    for dc in range(KC):
        for bri in range(3):
            wBr[dc * 3 + bri] = wbr_l[bri][:, dc * 128:(dc + 1) * 128].reshape(4, 128, 128).transpose(1, 0, 2)
    wOut = np.ascontiguousarray(wout_l.reshape(KC, 128, D // OC, OC).transpose(2, 1, 0, 3))
    return {"wZG": wZG, "bZG": bias, "wBr": wBr, "wOut": wOut,
            "lng": np.ascontiguousarray(lng_l[None, :]), "lnb": np.ascontiguousarray(lnb_l[None, :])}


def host_C_inputs(x_l, o_all, cw, TPC):
    NTOK = x_l.shape[0]
    o_full = np.stack([np.asarray(o) for o in o_all], axis=1)
    o_full = o_full.transpose(0, 1, 3, 2, 4).reshape(3, 8 * 65, NTOK)
    maps = []
    for c in range(NCORES):
        sl = slice(c * TPC, (c + 1) * TPC)
        m = dict(cw)
        m["xT"] = np.ascontiguousarray(x_l[sl].T)
        m["xtm"] = np.ascontiguousarray(x_l[sl])
        m["oT3"] = np.ascontiguousarray(o_full[:, :, sl])
        maps.append(m)
    return maps


_CACHE = {}


def _prog(key, fn):
    if key not in _CACHE:
        _CACHE[key] = fn()
    return _CACHE[key]


def kernel(x, w_in, b_in, swa_sinks, w_branch_fox, w_branch_swa, w_branch_moba, w_out, ln_gain, ln_bias):
    x = np.asarray(x, np.float32)
    NBATCH, S, _ = x.shape
    NTOK = NBATCH * S
    TPC = NTOK // NCORES
    cores = list(range(NCORES))
    xl = np.ascontiguousarray(x.reshape(NTOK, D))
    w_in, b_in = np.asarray(w_in, np.float32), np.asarray(b_in, np.float32)
    wbrs = [np.asarray(w, np.float32) for w in (w_branch_fox, w_branch_swa, w_branch_moba)]
    w_out, ln_gain, ln_bias = (np.asarray(a, np.float32) for a in (w_out, ln_gain, ln_bias))
    swa_sinks = np.asarray(swa_sinks, np.float32)
    for l in range(DEPTH):
        ncA = _prog(("A", TPC), lambda: build_A(TPC))
        rA = run_bass_kernel_spmd(ncA, host_A_inputs(xl, w_in[l], b_in[l], TPC), core_ids=cores).results
        qk_full = np.concatenate([np.asarray(r["qkT"]) for r in rA], axis=1)
        f_full = np.concatenate([np.asarray(r["fT"]) for r in rA], axis=1)
        v_full = np.concatenate([np.asarray(r["vtm"]) for r in rA], axis=0)
        del rA
        ncB = _prog(("B", S, NBATCH), lambda: build_B(S, NBATCH))
        rB = run_bass_kernel_spmd(ncB, host_B_inputs(qk_full, f_full, v_full, swa_sinks[l], S, NBATCH),
                                  core_ids=cores).results
        o_all = [np.asarray(r["oT"]) for r in rB]
        del rB, qk_full, f_full, v_full
        ncC = _prog(("C", TPC), lambda: build_C(TPC))
        cw = host_C_weights(w_in[l], b_in[l], [w[l] for w in wbrs], w_out[l], ln_gain[l], ln_bias[l])
        rC = run_bass_kernel_spmd(ncC, host_C_inputs(xl, o_all, cw, TPC), core_ids=cores).results
        xl = np.concatenate([np.asarray(r["y"]) for r in rC], axis=0)
        del rC, o_all
    return xl.reshape(NBATCH, S, D).astype(np.float32)
```

```python
import contextlib
import numpy as np
import ml_dtypes
import concourse.bass as bass
import concourse.mybir as mybir
from concourse.bass_utils import run_bass_kernel_spmd

F32 = mybir.dt.float32
BF16 = mybir.dt.bfloat16
AF = mybir.ActivationFunctionType
ALU = mybir.AluOpType
AX = mybir.AxisListType

NCORES = 8
D = 2048
KC = D // 128
HD = 64
NH = 8
DEPTH = 2
ALPHA = (2.0 * DEPTH) ** 0.25
LN_EPS = 1e-5
NEGBIG = -30000.0
SEM_ROLL = 30000

SEG = {}
_o = 0
for _n, _s in (("fox_q", 512), ("fox_k", 512), ("fox_v", 512), ("fox_z", 512), ("fox_f", 8),
               ("swa_q", 512), ("swa_k", 128), ("swa_v", 128), ("swa_z", 512),
               ("moba_q", 512), ("moba_k", 512), ("moba_v", 512), ("moba_z", 512),
               ("gate_fox", D), ("gate_swa", D), ("gate_moba", D)):
    SEG[_n] = (_o, _s)
    _o += _s
N_IN = _o


class Buf:
    __slots__ = ("name", "w", "r")

    def __init__(self, name=""):
        self.name = name
        self.w = None
        self.r = {}


class DSem:
    def __init__(self, prog, name):
        self.h = prog.ctx.enter_context(prog.nc.semaphore(name))
        self.count = 0
        self.key = ("d", name)


class Prog:
    def __init__(self, nc, ctx, same_engine_sync=True):
        self.nc = nc
        self.ctx = ctx
        self.same_sync = same_engine_sync
        self.eng = {}
        self.semh = {}
        for nm, h in (("pe", nc.tensor), ("act", nc.scalar), ("dve", nc.vector),
                      ("pool", nc.gpsimd), ("sp", nc.sync)):
            self.eng[nm] = {"h": h, "name": nm, "gen": 0, "count": 0, "waited": {}}
            self._newsem(self.eng[nm])
        self.dsems = []
        self.n_inst = 0

    def _newsem(self, e):
        e["gen"] += 1
        e["count"] = 0
        e["key"] = ("e", e["name"], e["gen"])
        e["sem"] = self.ctx.enter_context(self.nc.semaphore(f"s_{e['name']}{e['gen']}"))
        self.semh[e["key"]] = e["sem"]

    def dsem(self, name=None):
        d = DSem(self, name or f"dq{len(self.dsems)}")
        self.dsems.append(d)
        self.semh[d.key] = d.h
        return d

    def _wait(self, e, key, val):
        if val <= e["waited"].get(key, 0):
            return
        e["waited"][key] = val
        e["h"].wait_ge(self.semh[key], val)
        self.n_inst += 1

    def _deps(self, e, reads, writes, same_ok, skip=None):
        k = e["key"]
        for b in reads:
            if b.w is not None and not (same_ok and b.w[0] == k) and b.w[0] != skip:
                self._wait(e, *b.w)
        for b in writes:
            if b.w is not None and not (same_ok and b.w[0] == k) and b.w[0] != skip:
                self._wait(e, *b.w)
            for rk, rv in b.r.items():
                if not (same_ok and rk == k) and rk != skip:
                    self._wait(e, rk, rv)

    @staticmethod
    def _mark(tok, reads, writes):
        for b in reads:
            if b.r.get(tok[0], 0) < tok[1]:
                b.r[tok[0]] = tok[1]
        for b in writes:
            b.w = tok
            b.r = {}

    def op(self, eng, fn, reads=(), writes=(), same_ok=None):
        e = self.eng[eng]
        if same_ok is None:
            same_ok = (eng == "pe") or not self.same_sync
        if e["count"] >= SEM_ROLL:
            self._newsem(e)
        self._deps(e, reads, writes, same_ok)
        ins = fn()
        if isinstance(ins, (list, tuple)):
            self.n_inst += len(ins)
            ins = ins[-1]
        else:
            self.n_inst += 1
        e["count"] += 1
        ins.then_inc(e["sem"], 1)
        tok = (e["key"], e["count"])
        self._mark(tok, reads, writes)
        return tok

    def dma(self, q, out, in_, dsem, reads=(), writes=(), **kw):
        e = self.eng[q]
        self._deps(e, reads, writes, False, skip=dsem.key)
        ins = e["h"].dma_start(out=out, in_=in_, **kw)
        dsem.count += 16
        ins.then_inc(dsem.h, 16)
        self.n_inst += 1
        tok = (dsem.key, dsem.count)
        self._mark(tok, reads, writes)
        return tok

    def finish(self, eng="sp"):
        e = self.eng[eng]
        for d in self.dsems:
            if d.count:
                self._wait(e, d.key, d.count)


def _sb(nc, ctx, name, shape, dt):
    return ctx.enter_context(nc.sbuf_tensor("sb_" + name, list(shape), dt))


def _ps(nc, ctx, name, shape, dt=F32):
    return ctx.enter_context(nc.psum_tensor("pp_" + name, list(shape), dt))


FM_SEGS = ("fox_q", "fox_k", "swa_q", "moba_q", "moba_k", "swa_k")
TM_SEGS = ("fox_v", "swa_v", "moba_v")
QK_ROW = {}
_o = 0
for _n in FM_SEGS:
    QK_ROW[_n] = _o
    _o += SEG[_n][1]
NQK = _o
V_COL = {}
_o = 0
for _n in TM_SEGS:
    V_COL[_n] = _o
    _o += SEG[_n][1]
NTM = _o
NA = NQK + 8 + NTM


def a_col_order():
    cols = []
    for n in FM_SEGS:
        cols += list(range(SEG[n][0], SEG[n][0] + SEG[n][1]))
    cols += list(range(SEG["fox_f"][0], SEG["fox_f"][0] + 8))
    for n in TM_SEGS:
        cols += list(range(SEG[n][0], SEG[n][0] + SEG[n][1]))
    return np.array(cols)


def build_A(TPC):
    nc = bass.Bass("TRN2", target_bir_lowering=False)
    xT = nc.dram_tensor("xT", [D, TPC], F32, kind="ExternalInput").ap()
    wA = nc.dram_tensor("wA", [D, NA], F32, kind="ExternalInput").ap()
    nblk = (NQK + 127) // 128 + 1
    bfm = nc.dram_tensor("bfm", [128, nblk], F32, kind="ExternalInput").ap()
    btm = nc.dram_tensor("btm", [1, NTM], F32, kind="ExternalInput").ap()
    qkT = nc.dram_tensor("qkT", [NQK, TPC], BF16, kind="ExternalOutput").ap()
    fT = nc.dram_tensor("fT", [8, TPC], F32, kind="ExternalOutput").ap()
    vtm = nc.dram_tensor("vtm", [TPC, NTM], BF16, kind="ExternalOutput").ap()
    HT = min(2048, TPC)
    groups = []
    c = 0
    while c < NQK:
        n = min(512, NQK - c)
        groups.append((c, n, "fm"))
        c += n
    groups.append((NQK, 8, "f"))
    c = NQK + 8
    while c < NA:
        n = min(512, NA - c)
        groups.append((c, n, "tm"))
        c += n
    with contextlib.ExitStack() as ctx:
        P = Prog(nc, ctx)
        xh = _sb(nc, ctx, "xh", [128, KC, HT], BF16)
        b_xh = Buf("xh")
        d_xh = P.dsem("d_xh")
        wb = [_sb(nc, ctx, f"wb{i}", [128, KC, 512], BF16) for i in range(2)]
        b_wb = [Buf(f"wb{i}") for i in range(2)]
        d_wb = [P.dsem(f"d_wb{i}") for i in range(2)]
        bfm_sb = _sb(nc, ctx, "bfm_sb", [128, nblk], F32)
        btm_sb = _sb(nc, ctx, "btm_sb", [128, NTM], F32)
        b_bias = Buf("bias")
        d_bias = P.dsem("d_bias")
        NST = 4
        st = [_sb(nc, ctx, f"st{i}", [128, 512], BF16) for i in range(NST)]
        stf = [_sb(nc, ctx, f"stf{i}", [8, 512], F32) for i in range(2)]
        b_st = [Buf(f"st{i}") for i in range(NST)]
        b_stf = [Buf(f"stf{i}") for i in range(2)]
        d_st = [P.dsem(f"d_st{i}") for i in range(NST)]
        d_stf = [P.dsem(f"d_stf{i}") for i in range(2)]
        NPS = 4
        ps = [_ps(nc, ctx, f"ps{i}", [128, 512]) for i in range(NPS)]
        b_ps = [Buf(f"ps{i}") for i in range(NPS)]
        P.dma("sp", bfm_sb[:], bfm, d_bias, writes=[b_bias])
        P.dma("sp", btm_sb[:], btm.partition_broadcast(128), d_bias, writes=[b_bias])
        ips = 0
        ist = 0
        istf = 0
        gi = 0
        for half in range(TPC // HT):
            t0 = half * HT
            for kc in range(KC):
                P.dma("pool", xh[:, kc, :], xT[kc * 128:(kc + 1) * 128, t0:t0 + HT], d_xh, writes=[b_xh])
            for (c0, ncols, kind) in groups:
                w_i = gi % 2
                gi += 1
                P.dma("pool", wb[w_i][:, :, 0:ncols],
                      wA[:, c0:c0 + ncols].rearrange("(kc p) n -> p kc n", p=128),
                      d_wb[w_i], writes=[b_wb[w_i]])
                for tt in range(HT // 512):
                    tok0 = t0 + tt * 512
                    xs = slice(tt * 512, (tt + 1) * 512)
                    if kind in ("fm", "f"):
                        for b0 in range(0, ncols, 128):
                            bw = min(128, ncols - b0)
                            pb = ips % NPS
                            ips += 1
                            blk = (c0 + b0) // 128

                            def mm(pb=pb, bw=bw, b0=b0, w_i=w_i, xs=xs):
                                return [nc.tensor.matmul(ps[pb][0:bw, :], wb[w_i][:, kc, b0:b0 + bw],
                                                         xh[:, kc, xs], start=(kc == 0), stop=(kc == KC - 1))
                                        for kc in range(KC)]
                            P.op("pe", mm, reads=[b_wb[w_i], b_xh], writes=[b_ps[pb]])
                            if kind == "fm":
                                si = ist % NST
                                ist += 1
                                P.op("act", lambda pb=pb, bw=bw, si=si, blk=blk: nc.scalar.activation(
                                    st[si][0:bw, :], ps[pb][0:bw, :], AF.Identity, bias=bfm_sb[0:bw, blk:blk + 1]),
                                    reads=[b_ps[pb], b_bias], writes=[b_st[si]])
                                P.dma("sp", qkT[c0 + b0:c0 + b0 + bw, tok0:tok0 + 512], st[si][0:bw, :], d_st[si],
                                      reads=[b_st[si]])
                            else:
                                si = istf % 2
                                istf += 1
                                P.op("act", lambda pb=pb, si=si, blk=blk: nc.scalar.activation(
                                    stf[si][:, :], ps[pb][0:8, :], AF.Identity, bias=bfm_sb[0:8, blk:blk + 1]),
                                    reads=[b_ps[pb], b_bias], writes=[b_stf[si]])
                                P.dma("sp", fT[:, tok0:tok0 + 512], stf[si][:, :], d_stf[si], reads=[b_stf[si]])
                    else:
                        vc0 = c0 - NQK - 8
                        for sub in range(4):
                            pb = ips % NPS
                            ips += 1
                            ts_ = slice(tt * 512 + sub * 128, tt * 512 + (sub + 1) * 128)

                            def mm(pb=pb, w_i=w_i, ts_=ts_, ncols=ncols):
                                return [nc.tensor.matmul(ps[pb][:, 0:ncols], xh[:, kc, ts_], wb[w_i][:, kc, 0:ncols],
                                                         start=(kc == 0), stop=(kc == KC - 1)) for kc in range(KC)]
                            P.op("pe", mm, reads=[b_wb[w_i], b_xh], writes=[b_ps[pb]])
                            si = ist % NST
                            ist += 1
                            P.op("dve", lambda pb=pb, si=si, ncols=ncols, vc0=vc0: nc.vector.tensor_tensor(
                                st[si][:, 0:ncols], ps[pb][:, 0:ncols], btm_sb[:, vc0:vc0 + ncols], ALU.add),
                                reads=[b_ps[pb], b_bias], writes=[b_st[si]])
                            P.dma("sp", vtm[tok0 + sub * 128:tok0 + (sub + 1) * 128, vc0:vc0 + ncols],
                                  st[si][:, 0:ncols], d_st[si], reads=[b_st[si]])
        P.finish("sp")
        print("build_A n_inst", P.n_inst)
    return nc


def host_A_inputs(x_l, w_in_l, b_in_l, TPC):
    order = a_col_order()
    wA = np.ascontiguousarray(w_in_l[:, order])
    bA = b_in_l[order]
    nblk = (NQK + 127) // 128 + 1
    bfm = np.zeros((128, nblk), np.float32)
    for blk in range(NQK // 128):
        bfm[:, blk] = bA[blk * 128:(blk + 1) * 128]
    bfm[0:8, NQK // 128] = bA[NQK:NQK + 8]
    btm = np.ascontiguousarray(bA[NQK + 8:][None, :])
    maps = []
    for c in range(NCORES):
        xs = x_l[c * TPC:(c + 1) * TPC]
        maps.append({"xT": np.ascontiguousarray(xs.T), "wA": wA, "bfm": bfm, "btm": btm})
    return maps


PSBIG = 240000.0


def b_consts(S):
    NKC = S // 128
    bf = ml_dtypes.bfloat16
    k_ = np.arange(128)[:, None]
    q_ = np.arange(512)[None, :]
    cmask = np.zeros((128, 4, 512), np.float32)
    for j in range(4):
        cmask[:, j, :] = np.where(q_ >= 128 * j + k_, 0.0, -PSBIG)
    q1 = np.arange(128)[None, :]
    smask = np.zeros((128, 2, 128), np.float32)
    smask[:, 0, :] = np.where(q1 >= k_, 0.0, -PSBIG)
    smask[:, 1, :] = np.where(q1 < k_, 0.0, -PSBIG)
    onehot = (np.arange(S)[None, :] // 256 == np.arange(64)[:, None]).astype(np.float32)
    tri = (np.arange(128)[:, None] < np.arange(128)[None, :]).astype(np.float32)
    shared = {"cmask": cmask.astype(bf), "smask": smask.astype(bf), "onehot": onehot.astype(bf),
              "identb": np.eye(128, dtype=np.float32).astype(bf), "identf": np.eye(128, dtype=np.float32),
              "tri": tri}
    per_core = []
    for c in range(NCORES):
        slope = 2.0 ** (-(c + 1))
        pos = np.arange(S, dtype=np.float64)
        mkb = (slope * pos).reshape(NKC, 128).T.astype(np.float32)
        aq = (-8.0 * slope * pos)
        maq = (aq - PSBIG).reshape(NKC, 128).T.astype(np.float32)
        skb = np.stack([slope * np.arange(128), slope * np.arange(128) - 128.0 * slope], axis=1).astype(np.float32)
        sqrow = (-8.0 * slope * (np.arange(S) % 128)).astype(np.float32)[None, :].astype(bf)
        per_core.append({"mkb": mkb, "maq": maq, "skb": skb, "sqrow": sqrow})
    return shared, per_core


def build_B(S, NBATCH=2, do=("fox", "moba", "swa")):
    NKC = S // 128
    NQB = S // 512
    NB = S // 256
    nc = bass.Bass("TRN2", target_bir_lowering=False)

    def din(name, shape, dt):
        return nc.dram_tensor(name, list(shape), dt, kind="ExternalInput").ap()
    qT = {k: din(k + "_q", [NBATCH, 64, S], BF16) for k in ("fox", "swa", "moba")}
    kT = {k: din(k + "_k", [NBATCH, 64, S], BF16) for k in ("fox", "swa", "moba")}
    vv = {k: din(k + "_v", [NBATCH, S, 64], BF16) for k in ("fox", "swa", "moba")}
    ff = din("fox_f", [NBATCH, S], F32)
    sink = din("sink", [1, 1], F32)
    cmask_d = din("cmask", [128, 4, 512], BF16)
    smask_d = din("smask", [128, 2, 128], BF16)
    onehot_d = din("onehot", [64, S], BF16)
    identb_d = din("identb", [128, 128], BF16)
    identf_d = din("identf", [128, 128], F32)
    tri_d = din("tri", [128, 128], F32)
    mkb_d = din("mkb", [128, NKC], F32)
    maq_d = din("maq", [128, NKC], F32)
    skb_d = din("skb", [128, 2], F32)
    sqrow_d = din("sqrow", [1, S], BF16)
    oT = nc.dram_tensor("oT", [3, NBATCH, 65, S], F32, kind="ExternalOutput").ap()
    scr = nc.dram_tensor("scr_sig", [NBATCH, S], BF16).ap()
    KIDX = {"fox": 0, "swa": 1, "moba": 2}

    with contextlib.ExitStack() as ctx:
        P = Prog(nc, ctx)
        sb = lambda n, s, d: _sb(nc, ctx, n, s, d)
        KT = [sb(f"KT{i}", [128, S], BF16) for i in range(2)]
        QT = [sb(f"QT{i}", [128, S], BF16) for i in range(2)]
        VA = [sb(f"VA{i}", [128, NKC, 65], BF16) for i in range(2)]
        tab = [sb(f"tab{i}", [128, NKC], F32) for i in range(2)]
        b_KT = [Buf() for _ in range(2)]
        b_QTq = [Buf() for _ in range(2)]
        b_QTa = [Buf() for _ in range(2)]
        b_VA = [Buf() for _ in range(2)]
        b_tab = [Buf() for _ in range(2)]
        d_KT = [P.dsem(f"d_KT{i}") for i in range(2)]
        d_QT = [P.dsem(f"d_QT{i}") for i in range(2)]
        d_QTa = [P.dsem(f"d_QTa{i}") for i in range(2)]
        d_VA = [P.dsem(f"d_VA{i}") for i in range(2)]
        cmask = sb("cmask", [128, 4, 512], BF16)
        smask = sb("smask", [128, 2, 128], BF16)
        identb = sb("identb", [128, 128], BF16)
        identf = sb("identf", [128, 128], F32)
        tri = sb("tri", [128, 128], F32)
        onesf = sb("onesf", [128, 128], F32)
        mkb = sb("mkb", [128, NKC], F32)
        maq = sb("maq", [128, NKC], F32)
        skb = sb("skb", [128, 2], F32)
        esink = sb("esink", [128, 1], F32)
        b_const = Buf("const")
        d_const = P.dsem("d_const")
        for dst, src in ((cmask, cmask_d), (smask, smask_d), (identb, identb_d), (identf, identf_d),
                         (tri, tri_d), (mkb, mkb_d), (maq, maq_d), (skb, skb_d)):
            P.dma("sp", dst[:], src, d_const, writes=[b_const])
        P.dma("sp", esink[64:65, :], sink, d_const, writes=[b_const])
        for i in range(2):
            P.dma("sp", KT[i][64:128, :], onehot_d, d_const, writes=[b_const])
        b_ones, b_VA1, b_esink = Buf("ones"), Buf("VA1"), Buf("esink")
        P.op("dve", lambda: nc.vector.memset(onesf[:], 1.0), writes=[b_ones])
        P.op("dve", lambda: [nc.vector.memset(VA[i][:, :, 64:65], 1.0) for i in range(2)], writes=[b_VA1])
        P.op("act", lambda: nc.scalar.activation(esink[64:65, :], esink[64:65, :], AF.Exp),
             reads=[b_const], writes=[b_esink])

        NSB = 4
        psS = [_ps(nc, ctx, f"psS{i}", [128, 512]) for i in range(NSB)]
        b_psS = [Buf(f"psS{i}") for i in range(NSB)]
        psO = [_ps(nc, ctx, f"psO{i}", [128, 512]) for i in range(2)]
        b_psO = [Buf(f"psO{i}") for i in range(2)]
        psM = _ps(nc, ctx, "psM", [128, 512])
        b_psM = Buf("psM")
        psT = _ps(nc, ctx, "psT", [128, 128], BF16)
        b_psT = Buf("psT")
        PT = [sb(f"PT{i}", [128, 512], BF16) for i in range(NSB)]
        b_PT = [Buf(f"PT{i}") for i in range(NSB)]
        NOST = 3
        ost = [sb(f"ost{i}", [65, 512], F32) for i in range(NOST)]
        b_ost = [Buf(f"ost{i}") for i in range(NOST)]
        d_ost = [P.dsem(f"d_ost{i}") for i in range(NOST)]
        fkc = sb("fkc", [128, 128], F32)
        fcs = sb("fcs", [128, 128], F32)
        foff = sb("foff", [128, 1], F32)
        fsig = sb("fsig", [128, 128], BF16)
        b_f = Buf("f")
        d_f = P.dsem("d_f")
        d_scr = P.dsem("d_scr")
        b_scr = Buf("scr")
        km = sb("km", [64, 64], F32)
        kmh = sb("kmh", [64, 64], BF16)
        kmhf = sb("kmhf", [64, 64], F32)
        kml = sb("kml", [64, 64], BF16)
        b_km = Buf("km")
        work = sb("work", [128, 64], F32)
        top8 = sb("top8", [128, 8], F32)
        sel = sb("sel", [128, 64], F32)
        MT = sb("MT", [128, 128], BF16)
        b_work, b_top8, b_sel, b_MT = Buf("work"), Buf("top8"), Buf("sel"), Buf("MT")
        st = {"ips": 0, "io": 0, "iost": 0}

        def pro_loads(kind, b, s):
            ops = []
            ops.append(lambda: P.dma("sp", KT[s][0:64, :], kT[kind][b], d_KT[s], writes=[b_KT[s]]))
            ops.append(lambda: P.dma("sp", QT[s][0:64, :], qT[kind][b], d_QT[s], writes=[b_QTq[s]]))
            vsrc = vv[kind][b].rearrange("(kc p) d -> p kc d", p=128)
            nq = 4 if NKC >= 4 else 1
            for qi in range(nq):
                ks = slice(qi * (NKC // nq), (qi + 1) * (NKC // nq))
                ops.append(lambda ks=ks: P.dma("pool", VA[s][:, ks, 0:64], vsrc[:, ks, :], d_VA[s], writes=[b_VA[s]]))
            return ops

        def pro_fox(b, s):
            dm = pro_loads("fox", b, s)
            dm.append(lambda: P.dma("sp", fkc[0:NKC, :], ff[b].rearrange("(kc p) -> kc p", p=128), d_f, writes=[b_f]))
            cp = []
            cp.append(lambda: P.op("act", lambda: nc.scalar.activation(fkc[0:NKC, :], fkc[0:NKC, :], AF.Exp, scale=-1.0),
                                   reads=[b_f], writes=[b_f]))
            cp.append(lambda: P.op("act", lambda: nc.scalar.activation(fkc[0:NKC, :], fkc[0:NKC, :], AF.Ln, bias=1.0),
                                   reads=[b_f], writes=[b_f]))
            cp.append(lambda: P.op("dve", lambda: nc.vector.tensor_tensor_scan(fcs[0:NKC, :], onesf[0:NKC, :], fkc[0:NKC, :],
                                                                               0.0, ALU.mult, ALU.add),
                                   reads=[b_f, b_ones], writes=[b_f]))
            cp.append(lambda: P.op("pe", lambda: nc.tensor.matmul(psM[0:NKC, 0:1], tri[0:NKC, 0:NKC], fcs[0:NKC, 127:128],
                                                                  start=True, stop=True),
                                   reads=[b_f, b_const], writes=[b_psM]))
            cp.append(lambda: P.op("dve", lambda: nc.vector.tensor_copy(foff[0:NKC, :], psM[0:NKC, 0:1]),
                                   reads=[b_psM], writes=[b_f]))
            cp.append(lambda: P.op("dve", lambda: nc.vector.tensor_scalar(fcs[0:NKC, :], fcs[0:NKC, :], foff[0:NKC, 0:1], None,
                                                                          ALU.add), reads=[b_f], writes=[b_f]))
            cp.append(lambda: P.op("dve", lambda: nc.vector.tensor_scalar(fsig[0:NKC, :], fcs[0:NKC, :], -8.0, None, ALU.mult),
                                   reads=[b_f], writes=[b_f]))
            cp.append(lambda: P.dma("sp", scr[b].rearrange("(kc p) -> kc p", p=128), fsig[0:NKC, :], d_scr,
                                    reads=[b_f], writes=[b_scr]))
            cp.append(lambda: P.dma("sp", QT[s][64:128, :], scr[b].partition_broadcast(64), d_QTa[s],
                                    reads=[b_scr], writes=[b_QTa[s]]))
            cp.append(lambda: P.op("pe", lambda: nc.tensor.transpose(psM[:, 0:NKC], fcs[0:NKC, :], identf[0:NKC, 0:NKC]),
                                   reads=[b_f, b_const], writes=[b_psM]))
            cp.append(lambda: P.op("dve", lambda: nc.vector.tensor_copy(tab[s][:, :], psM[:, 0:NKC]),
                                   reads=[b_psM], writes=[b_tab[s]]))
            return dm, cp

        def pro_swa(b, s):
            dm = pro_loads("swa", b, s)
            dm.append(lambda: P.dma("sp", QT[s][64:128, :], sqrow_d[0].partition_broadcast(64), d_QTa[s],
                                    writes=[b_QTa[s]]))
            return dm, []

        def pro_moba(b, s):
            dm = pro_loads("moba", b, s)
            cp = []
            cp.append(lambda: P.op("dve", lambda: nc.vector.tensor_reduce(
                km[:, 0:NB], KT[s][0:64, :].rearrange("d (n k) -> d n k", k=256), AX.X, ALU.add),
                reads=[b_KT[s]], writes=[b_km]))
            cp.append(lambda: P.op("dve", lambda: nc.vector.tensor_scalar(km[:, 0:NB], km[:, 0:NB], 1.0 / 256.0, None, ALU.mult),
                                   reads=[b_km], writes=[b_km]))
            cp.append(lambda: P.op("dve", lambda: nc.vector.tensor_copy(kmh[:, 0:NB], km[:, 0:NB]), reads=[b_km], writes=[b_km]))
            cp.append(lambda: P.op("dve", lambda: nc.vector.tensor_copy(kmhf[:, 0:NB], kmh[:, 0:NB]), reads=[b_km], writes=[b_km]))
            cp.append(lambda: P.op("dve", lambda: nc.vector.tensor_tensor(kml[:, 0:NB], km[:, 0:NB], kmhf[:, 0:NB], ALU.subtract),
                                   reads=[b_km], writes=[b_km]))
            cp.append(lambda: P.op("dve", lambda: nc.vector.memset(work[:, :], -1e30), writes=[b_work]))
            cp.append(lambda: P.op("dve", lambda: nc.vector.memset(sel[:, :], 0.0), writes=[b_sel]))
            cp.append(lambda: P.op("dve", lambda: nc.vector.memset(MT[:, :], 0.0), writes=[b_MT]))
            cp.append(lambda: P.op("dve", lambda: nc.vector.tensor_copy(tab[s][:, :], mkb[:, :]),
                                   reads=[b_const], writes=[b_tab[s]]))
            for t in range(NKC):
                nb = t // 2
                qs = slice(t * 128, (t + 1) * 128)
                if nb > 0:
                    cp.append(lambda qs=qs: P.op("pe", lambda: [
                        nc.tensor.matmul(psM[:, 0:NB], QT[s][0:64, qs], kmh[:, 0:NB], start=True, stop=False),
                        nc.tensor.matmul(psM[:, 0:NB], QT[s][0:64, qs], kml[:, 0:NB], start=False, stop=True)],
                        reads=[b_QTq[s], b_km], writes=[b_psM]))
                    cp.append(lambda nb=nb: P.op("dve", lambda: nc.vector.tensor_copy(work[:, 0:nb], psM[:, 0:nb]),
                                                 reads=[b_psM], writes=[b_work]))
                    cp.append(lambda: P.op("dve", lambda: nc.vector.max(top8[:, :], work[:, 0:max(NB, 8)]),
                                           reads=[b_work], writes=[b_top8]))
                    cp.append(lambda nb=nb: P.op("dve", lambda: nc.vector.tensor_scalar(
                        sel[:, 0:nb], work[:, 0:nb], top8[:, 2:3], None, ALU.is_ge),
                        reads=[b_work, b_top8], writes=[b_sel]))
                if t % 2 == 0:
                    cp.append(lambda nb=nb: P.op("dve", lambda: nc.vector.memset(sel[:, nb:nb + 1], 1.0), writes=[b_sel]))
                cp.append(lambda t=t: P.op("dve", lambda: nc.vector.tensor_scalar(
                    MT[:, 64:128], sel[:, :], PSBIG, maq[:, t:t + 1], ALU.mult, ALU.add),
                    reads=[b_sel, b_const], writes=[b_MT]))
                cp.append(lambda: P.op("pe", lambda: nc.tensor.transpose(psT[:, :], MT[:, :], identb[:, :]),
                                       reads=[b_MT, b_const], writes=[b_psT]))
                cp.append(lambda qs=qs: P.op("dve", lambda: nc.vector.tensor_copy(QT[s][64:128, qs], psT[64:128, :]),
                                             reads=[b_psT], writes=[b_QTa[s]]))
            return dm, cp

        PRO = {"fox": pro_fox, "swa": pro_swa, "moba": pro_moba}

        class Feeder:
            def __init__(self, dm, cp, delay, rate):
                self.dm, self.cp, self.delay, self.rate, self.n = list(dm), list(cp), delay, rate, 0

            def tick(self):
                if self.n == 0:
                    for f in self.dm:
                        f()
                    self.dm = []
                self.n += 1
                if self.n > self.delay:
                    for _ in range(self.rate):
                        if self.cp:
                            self.cp.pop(0)()

            def flush(self):
                for f in self.dm:
                    f()
                for f in self.cp:
                    f()
                self.dm, self.cp = [], []

        def epilogue(kidx, b, ob, q0, nq, add_sink):
            oi = st["iost"] % NOST
            st["iost"] += 1
            P.op("dve", lambda: nc.vector.tensor_copy(ost[oi][:, 0:nq], psO[ob][0:65, 0:nq]),
                 reads=[b_psO[ob]], writes=[b_ost[oi]])
            if add_sink:
                P.op("dve", lambda: nc.vector.tensor_scalar(ost[oi][64:65, 0:nq], ost[oi][64:65, 0:nq],
                                                            esink[64:65, 0:1], None, ALU.add),
                     reads=[b_ost[oi], b_esink], writes=[b_ost[oi]])
            P.dma("sp", oT[kidx, b, :, q0:q0 + nq], ost[oi][:, 0:nq], d_ost[oi], reads=[b_ost[oi]])

        LOOK = 2

        def dense_pass(kidx, b, s, feeder):
            tiles = [(qb, kc) for qb in range(NQB) for kc in range(4 * (qb + 1))]
            pend = []

            def do_pv():
                pqb, pkc, psi = pend.pop(0)
                ob = pqb % 2
                last = pkc == 4 * (pqb + 1) - 1
                P.op("pe", lambda: nc.tensor.matmul(psO[ob][0:65, :], VA[s][:, pkc, :], PT[psi][:, :],
                                                    start=(pkc == 0), stop=last),
                     reads=[b_VA[s], b_VA1, b_PT[psi]], writes=[b_psO[ob]])
                if last:
                    epilogue(kidx, b, ob, pqb * 512, 512, False)
            for (qb, kc) in tiles:
                si = st["ips"] % NSB
                st["ips"] += 1
                diag = kc >= 4 * qb
                j = kc - 4 * qb

                def qk(si=si, qb=qb, kc=kc, diag=diag, j=j):
                    r = [nc.tensor.matmul(psS[si][:, :], KT[s][:, kc * 128:(kc + 1) * 128],
                                          QT[s][:, qb * 512:(qb + 1) * 512], start=True, stop=not diag)]
                    if diag:
                        r.append(nc.tensor.matmul(psS[si][:, :], identb[:, :], cmask[:, j, :],
                                                  start=False, stop=True))
                    return r
                P.op("pe", qk, reads=[b_KT[s], b_QTq[s], b_QTa[s], b_const], writes=[b_psS[si]])
                P.op("act", lambda si=si, kc=kc: nc.scalar.activation(PT[si][:, :], psS[si][:, :], AF.Exp,
                                                                      bias=tab[s][:, kc:kc + 1], scale=0.125),
                     reads=[b_psS[si], b_tab[s]], writes=[b_PT[si]])
                pend.append((qb, kc, si))
                if len(pend) > LOOK:
                    do_pv()
                if feeder is not None:
                    feeder.tick()
            while pend:
                do_pv()

        def swa_pass(b, s, feeder):
            pend = []

            def do_pv():
                g, sa, sp_ = pend.pop(0)
                ob = st["io"] % 2
                st["io"] += 1

                def pv():
                    r = []
                    for jj in range(4):
                        blk = 4 * g + jj
                        cs = slice(jj * 128, (jj + 1) * 128)
                        if blk > 0:
                            r.append(nc.tensor.matmul(psO[ob][0:65, cs], VA[s][:, blk - 1, :], PT[sp_][:, cs],
                                                      start=True, stop=False))
                        r.append(nc.tensor.matmul(psO[ob][0:65, cs], VA[s][:, blk, :], PT[sa][:, cs],
                                                  start=(blk == 0), stop=True))
                    return r
                P.op("pe", pv, reads=[b_VA[s], b_VA1, b_PT[sa], b_PT[sp_]], writes=[b_psO[ob]])
                epilogue(1, b, ob, g * 512, 512, True)
            for g in range(NQB):
                sa = st["ips"] % NSB
                st["ips"] += 1
                sp_ = st["ips"] % NSB
                st["ips"] += 1

                def qk_own(sa=sa, g=g):
                    r = []
                    for jj in range(4):
                        blk = 4 * g + jj
                        cs = slice(jj * 128, (jj + 1) * 128)
                        r.append(nc.tensor.matmul(psS[sa][:, cs], KT[s][:, blk * 128:(blk + 1) * 128],
                                                  QT[s][:, blk * 128:(blk + 1) * 128], start=True, stop=False))
                        r.append(nc.tensor.matmul(psS[sa][:, cs], identb[:, :], smask[:, 0, :], start=False, stop=True))
                    return r

                def qk_prev(sp_=sp_, g=g):
                    r = []
                    for jj in range(4):
                        blk = 4 * g + jj
                        cs = slice(jj * 128, (jj + 1) * 128)
                        if blk > 0:
                            r.append(nc.tensor.matmul(psS[sp_][:, cs], KT[s][:, (blk - 1) * 128:blk * 128],
                                                      QT[s][:, blk * 128:(blk + 1) * 128], start=True, stop=False))
                        r.append(nc.tensor.matmul(psS[sp_][:, cs], identb[:, :], smask[:, 1, :],
                                                  start=(blk == 0), stop=True))
                    return r
                P.op("pe", qk_own, reads=[b_KT[s], b_QTq[s], b_QTa[s], b_const], writes=[b_psS[sa]])
                P.op("pe", qk_prev, reads=[b_KT[s], b_QTq[s], b_QTa[s], b_const], writes=[b_psS[sp_]])
                P.op("act", lambda sa=sa: nc.scalar.activation(PT[sa][:, :], psS[sa][:, :], AF.Exp,
                                                               bias=skb[:, 0:1], scale=0.125),
                     reads=[b_psS[sa], b_const], writes=[b_PT[sa]])
                P.op("act", lambda sp_=sp_: nc.scalar.activation(PT[sp_][:, :], psS[sp_][:, :], AF.Exp,
                                                                 bias=skb[:, 1:2], scale=0.125),
                     reads=[b_psS[sp_], b_const], writes=[b_PT[sp_]])
                pend.append((g, sa, sp_))
                if len(pend) > 1:
                    do_pv()
                if feeder is not None:
                    feeder.tick()
            while pend:
                do_pv()

        passes = [(k, b) for b in range(NBATCH) for k in do]
        dm0, cp0 = PRO[passes[0][0]](passes[0][1], 0)
        Feeder(dm0, cp0, 0, 1).flush()
        for p, (kind, b) in enumerate(passes):
            s = p % 2
            feeder = None
            if p + 1 < len(passes):
                nk, nb_ = passes[p + 1]
                dm, cp = PRO[nk](nb_, (p + 1) % 2)
                if kind == "swa":
                    feeder = Feeder(dm, cp, 8, 2)
                else:
                    feeder = Feeder(dm, cp, 64, 1)
            if kind == "swa":
                swa_pass(b, s, feeder)
            else:
                dense_pass(KIDX[kind], b, s, feeder)
            if feeder is not None:
                feeder.flush()
        P.finish("sp")
        print("build_B n_inst", P.n_inst)
    return nc


def host_B_inputs(qk_full, f_full, v_full, sinks_l, S, NBATCH=2):
    shared, per_core = b_consts(S)
    maps = []

    def qk(name, h):
        r0 = QK_ROW[name] + 64 * h
        return np.ascontiguousarray(qk_full[r0:r0 + 64].reshape(64, NBATCH, S).transpose(1, 0, 2))

    def vh(name, h):
        c0 = V_COL[name] + 64 * h
        return np.ascontiguousarray(v_full[:, c0:c0 + 64].reshape(NBATCH, S, 64))
    for c in range(NCORES):
        m = dict(shared)
        m.update(per_core[c])
        m["fox_q"], m["fox_k"], m["fox_v"] = qk("fox_q", c), qk("fox_k", c), vh("fox_v", c)
        m["moba_q"], m["moba_k"], m["moba_v"] = qk("moba_q", c), qk("moba_k", c), vh("moba_v", c)
        m["swa_q"], m["swa_k"], m["swa_v"] = qk("swa_q", c), qk("swa_k", c // 4), vh("swa_v", c // 4)
        m["fox_f"] = np.ascontiguousarray(f_full[c].reshape(NBATCH, S))
        m["sink"] = np.ascontiguousarray(sinks_l[c].reshape(1, 1)).astype(np.float32)
        maps.append(m)
    return maps


OC = 256
BR = ("fox", "swa", "moba")


def build_C(TPC):
    nc = bass.Bass("TRN2", target_bir_lowering=False)

    def din(name, shape, dt=F32):
        return nc.dram_tensor(name, list(shape), dt, kind="ExternalInput").ap()
    xT = din("xT", [D, TPC])
    xtm = din("xtm", [TPC, D])
    oT3 = din("oT3", [3, 8 * 65, TPC])
    wZG = din("wZG", [60, 128, KC, 128])
    bZG = din("bZG", [128, 60])
    wBr = din("wBr", [48, 128, 4, 128])
    wOut = din("wOut", [D // OC, 128, KC, OC])
    lng = din("lng", [1, D])
    lnb = din("lnb", [1, D])
    y = nc.dram_tensor("y", [TPC, D], F32, kind="ExternalOutput").ap()
    TT = min(1024, TPC)
    rdn = nc.dram_tensor("rdn_scr", [TPC // TT, 24, TT], F32).ap()
    NS = TT // 512
    HS = 512 // 128
    with contextlib.ExitStack() as ctx:
        P = Prog(nc, ctx)
        sb = lambda n, s, d: _sb(nc, ctx, n, s, d)
        xTt = sb("xTt", [128, KC, TT], BF16)
        ozT = sb("ozT", [128, 12, TT], BF16)
        yT = sb("yT", [128, KC, TT], BF16)
        b_xTt, b_yT = Buf("xTt"), Buf("yT")
        b_ozT = [Buf(f"ozT{i}") for i in range(3)]
        d_xTt = P.dsem("d_xTt")
        NW = 4
        wblk = [sb(f"wblk{i}", [128, KC, 128], BF16) for i in range(NW)]
        b_wblk = [Buf(f"wblk{i}") for i in range(NW)]
        d_wblk = [P.dsem(f"d_wblk{i}") for i in range(NW)]
        wbr = [sb(f"wbr{i}", [128, 4, 128], BF16) for i in range(3)]
        b_wbr = [Buf(f"wbr{i}") for i in range(3)]
        d_wbr = [P.dsem(f"d_wbr{i}") for i in range(3)]
        wo = [sb(f"wo{i}", [128, KC, OC], BF16) for i in range(2)]
        b_wo = [Buf(f"wo{i}") for i in range(2)]
        d_wo = [P.dsem(f"d_wo{i}") for i in range(2)]
        bzg = sb("bzg", [128, 60], F32)
        gbc = sb("gbc", [128, D], F32)
        bbc = sb("bbc", [128, D], F32)
        b_const = Buf("const")
        d_const = P.dsem("d_const")
        P.dma("sp", bzg[:], bZG, d_const, writes=[b_const])
        P.dma("sp", gbc[:], lng[0].partition_broadcast(128), d_const, writes=[b_const])
        P.dma("sp", bbc[:], lnb[0].partition_broadcast(128), d_const, writes=[b_const])
        ot = [sb(f"ot{i}", [128, 512], F32) for i in range(2)]
        rd = [sb(f"rd{i}", [128, 512], F32) for i in range(2)]
        dn = sb("dn", [24, TT], F32)
        b_dn = Buf("dn")
        d_dn = P.dsem("d_dn")
        NTILE = TPC // TT
        b_rdn = [Buf(f"rdn{i}") for i in range(NTILE)]
        d_rdn = P.dsem("d_rdn")
        den_view = oT3.rearrange("b (h r) t -> b h r t", r=65)
        b_ot = [Buf(f"ot{i}") for i in range(2)]
        d_ot = [P.dsem(f"d_ot{i}") for i in range(2)]
        sz = [sb(f"sz{i}", [128, 512], F32) for i in range(2)]
        b_sz = [Buf(f"sz{i}") for i in range(2)]
        sg = [sb(f"sg{i}", [128, 512], F32) for i in range(2)]
        b_sg = [Buf(f"sg{i}") for i in range(2)]
        yacc = [sb(f"yacc{i}", [128, 512], F32) for i in range(NS)]
        b_yacc = [Buf(f"yacc{i}") for i in range(NS)]
        tmp = sb("tmp", [128, 512], F32)
        b_tmp = Buf("tmp")
        r = [sb(f"r{i}", [128, D], F32) for i in range(HS)]
        b_r = [Buf(f"r{i}") for i in range(HS)]
        d_rl = [P.dsem(f"d_rl{i}") for i in range(HS)]
        d_rs = [P.dsem(f"d_rs{i}") for i in range(HS)]
        nst = D // 512
        stats = sb("stats", [128, nst, 6], F32)
        mv = sb("mv", [128, 2], F32)
        rstd = sb("rstd", [128, 1], F32)
        b_stats = Buf("stats")
        NPS = 6
        ps = [_ps(nc, ctx, f"ps{i}", [128, 512]) for i in range(NPS)]
        b_ps = [Buf(f"ps{i}") for i in range(NPS)]
        c = {"ps": 0, "w": 0, "ot": 0, "sz": 0, "sg": 0, "wo": 0}

        def nxt(k, n):
            v = c[k] % n
            c[k] += 1
            return v

        def proj16(pb, wi, s):
            return [nc.tensor.matmul(ps[pb][:, :], wblk[wi][:, kc, :], xTt[:, kc, s * 512:(s + 1) * 512],
                                     start=(kc == 0), stop=(kc == KC - 1)) for kc in range(KC)]

        def load_xT(tile):
            tk = tile * TT
            for kc in range(KC):
                P.dma("pool", xTt[:, kc, :], xT[kc * 128:(kc + 1) * 128, tk:tk + TT], d_xTt, writes=[b_xTt])

        def prep_den(tile):
            tk = tile * TT
            for bi in range(3):
                P.dma("sp", dn[bi * 8:(bi + 1) * 8, :], den_view[bi, :, 64, tk:tk + TT], d_dn, writes=[b_dn])
            P.op("dve", lambda: nc.vector.reciprocal(dn[:, :], dn[:, :]), reads=[b_dn], writes=[b_dn])
            P.dma("sp", rdn[tile], dn[:, :], d_rdn, reads=[b_dn], writes=[b_rdn[tile]])

        load_xT(0)
        prep_den(0)
        for tile in range(TPC // TT):
            tok0 = tile * TT
            for br in range(3):
                for wc in range(4):
                    zi = br * 4 + wc
                    wi = nxt("w", NW)
                    P.dma("pool", wblk[wi][:], wZG[zi], d_wblk[wi], writes=[b_wblk[wi]])
                    for s in range(NS):
                        pb = nxt("ps", NPS)
                        P.op("pe", lambda pb=pb, wi=wi, s=s: proj16(pb, wi, s),
                             reads=[b_wblk[wi], b_xTt], writes=[b_ps[pb]])
                        oi = nxt("ot", 2)
                        tsl = slice(tok0 + s * 512, tok0 + (s + 1) * 512)
                        for hh in range(2):
                            r0 = (2 * wc + hh) * 65
                            P.dma("sp", ot[oi][hh * 64:(hh + 1) * 64, :], oT3[br, r0:r0 + 64, tsl], d_ot[oi],
                                  writes=[b_ot[oi]])
                            P.dma("sp", rd[oi][hh * 64:(hh + 1) * 64, :],
                                  rdn[tile, br * 8 + 2 * wc + hh, s * 512:(s + 1) * 512].partition_broadcast(64),
                                  d_ot[oi], reads=[b_rdn[tile]], writes=[b_ot[oi]])
                        P.op("dve", lambda oi=oi: nc.vector.tensor_tensor(ot[oi][:], ot[oi][:], rd[oi][:], ALU.mult),
                             reads=[b_ot[oi]], writes=[b_ot[oi]])
                        zi_ = nxt("sz", 2)
                        P.op("act", lambda pb=pb, zi_=zi_, zi=zi: nc.scalar.activation(
                            sz[zi_][:], ps[pb][:], AF.Silu, bias=bzg[:, zi:zi + 1]),
                            reads=[b_ps[pb], b_const], writes=[b_sz[zi_]])
                        P.op("dve", lambda zi_=zi_, oi=oi, zi=zi, s=s: nc.vector.tensor_tensor(
                            ozT[:, zi, s * 512:(s + 1) * 512], sz[zi_][:], ot[oi][:], ALU.mult),
                            reads=[b_sz[zi_], b_ot[oi]], writes=[b_ozT[br]])
            for dc in range(KC):
                for br in range(3):
                    gi = 12 + dc * 3 + br
                    wi = nxt("w", NW)
                    P.dma("pool", wblk[wi][:], wZG[gi], d_wblk[wi], writes=[b_wblk[wi]])
                    P.dma("pool", wbr[br][:], wBr[dc * 3 + br], d_wbr[br], writes=[b_wbr[br]])
                    for s in range(NS):
                        pg = nxt("ps", NPS)
                        P.op("pe", lambda pg=pg, wi=wi, s=s: proj16(pg, wi, s),
                             reads=[b_wblk[wi], b_xTt], writes=[b_ps[pg]])
                        gi_ = nxt("sg", 2)
                        P.op("act", lambda pg=pg, gi_=gi_, gi=gi: nc.scalar.activation(
                            sg[gi_][:], ps[pg][:], AF.Sigmoid, bias=bzg[:, gi:gi + 1]),
                            reads=[b_ps[pg], b_const], writes=[b_sg[gi_]])
                        pbr = nxt("ps", NPS)
                        P.op("pe", lambda pbr=pbr, br=br, s=s: [
                            nc.tensor.matmul(ps[pbr][:, :], wbr[br][:, wc, :], ozT[:, br * 4 + wc, s * 512:(s + 1) * 512],
                                             start=(wc == 0), stop=(wc == 3)) for wc in range(4)],
                            reads=[b_wbr[br], b_ozT[br]], writes=[b_ps[pbr]])
                        ysl = yT[:, dc, s * 512:(s + 1) * 512]
                        if br == 0:
                            P.op("dve", lambda pbr=pbr, gi_=gi_, s=s: nc.vector.tensor_tensor(
                                yacc[s][:], ps[pbr][:], sg[gi_][:], ALU.mult),
                                reads=[b_ps[pbr], b_sg[gi_]], writes=[b_yacc[s]])
                        else:
                            P.op("dve", lambda pbr=pbr, gi_=gi_: nc.vector.tensor_tensor(
                                tmp[:], ps[pbr][:], sg[gi_][:], ALU.mult),
                                reads=[b_ps[pbr], b_sg[gi_]], writes=[b_tmp])
                            if br == 1:
                                P.op("dve", lambda s=s: nc.vector.tensor_tensor(yacc[s][:], yacc[s][:], tmp[:], ALU.add),
                                     reads=[b_yacc[s], b_tmp], writes=[b_yacc[s]])
                            else:
                                P.op("dve", lambda s=s, ysl=ysl: nc.vector.tensor_tensor(ysl, yacc[s][:], tmp[:], ALU.add),
                                     reads=[b_yacc[s], b_tmp], writes=[b_yT])
            for hf in range(NS):
                for sub in range(HS):
                    t0 = tok0 + hf * 512 + sub * 128
                    P.dma("sp", r[sub][:], xtm[t0:t0 + 128, :], d_rl[sub], writes=[b_r[sub]])
                for ob in range(D // OC):
                    wi = nxt("wo", 2)
                    P.dma("pool", wo[wi][:], wOut[ob], d_wo[wi], writes=[b_wo[wi]])
                    for sub in range(HS):
                        pb = nxt("ps", NPS)
                        tsl = slice(hf * 512 + sub * 128, hf * 512 + (sub + 1) * 128)
                        P.op("pe", lambda pb=pb, wi=wi, tsl=tsl: [
                            nc.tensor.matmul(ps[pb][:, 0:OC], yT[:, kc, tsl], wo[wi][:, kc, :],
                                             start=(kc == 0), stop=(kc == KC - 1)) for kc in range(KC)],
                            reads=[b_yT, b_wo[wi]], writes=[b_ps[pb]])
                        P.op("dve", lambda pb=pb, sub=sub, ob=ob: nc.vector.scalar_tensor_tensor(
                            r[sub][:, ob * OC:(ob + 1) * OC], r[sub][:, ob * OC:(ob + 1) * OC], float(ALPHA),
                            ps[pb][:, 0:OC], ALU.mult, ALU.add),
                            reads=[b_ps[pb], b_r[sub]], writes=[b_r[sub]])
                if hf == NS - 1 and tile + 1 < TPC // TT:
                    load_xT(tile + 1)
                    prep_den(tile + 1)
                for sub in range(HS):
                    t0 = tok0 + hf * 512 + sub * 128
                    P.op("dve", lambda sub=sub: [nc.vector.bn_stats(stats[:, i, :], r[sub][:, i * 512:(i + 1) * 512])
                                                 for i in range(nst)],
                         reads=[b_r[sub]], writes=[b_stats])
                    P.op("dve", lambda: nc.vector.bn_aggr(mv[:, :], stats[:].rearrange("p a b -> p (a b)")),
                         reads=[b_stats], writes=[b_stats])
                    P.op("act", lambda: nc.scalar.activation(rstd[:, :], mv[:, 1:2], AF.Sqrt, bias=float(LN_EPS)),
                         reads=[b_stats], writes=[b_stats])
                    P.op("dve", lambda: nc.vector.reciprocal(rstd[:, :], rstd[:, :]), reads=[b_stats], writes=[b_stats])
                    P.op("dve", lambda sub=sub: nc.vector.tensor_scalar(r[sub][:], r[sub][:], mv[:, 0:1], rstd[:, 0:1],
                                                                        ALU.subtract, ALU.mult),
                         reads=[b_r[sub], b_stats], writes=[b_r[sub]])
                    P.op("dve", lambda sub=sub: nc.vector.tensor_tensor(r[sub][:], r[sub][:], gbc[:], ALU.mult),
                         reads=[b_r[sub], b_const], writes=[b_r[sub]])
                    P.op("dve", lambda sub=sub: nc.vector.tensor_tensor(r[sub][:], r[sub][:], bbc[:], ALU.add),
                         reads=[b_r[sub], b_const], writes=[b_r[sub]])
                    P.dma("sp", y[t0:t0 + 128, :], r[sub][:], d_rs[sub], reads=[b_r[sub]])
        P.finish("sp")
        print("build_C n_inst", P.n_inst)
    return nc


def host_C_weights(w_in_l, b_in_l, wbr_l, wout_l, lng_l, lnb_l):
    blocks = []
    bias = np.zeros((128, 60), np.float32)
    bi = 0
    for br in BR:
        c0 = SEG[br + "_z"][0]
        for wc in range(4):
            blocks.append((c0 + wc * 128))
    for dc in range(KC):
        for br in BR:
            blocks.append(SEG["gate_" + br][0] + dc * 128)
    wZG = np.empty((60, 128, KC, 128), np.float32)
    for bi, c0 in enumerate(blocks):
        wZG[bi] = w_in_l[:, c0:c0 + 128].reshape(KC, 128, 128).transpose(1, 0, 2)
        bias[:, bi] = b_in_l[c0:c0 + 128]
    wBr = np.empty((48, 128, 4, 128), np.float32)
    for dc in range(KC):
        for bri in range(3):
            wBr[dc * 3 + bri] = wbr_l[bri][:, dc * 128:(dc + 1) * 128].reshape(4, 128, 128).transpose(1, 0, 2)
    wOut = np.ascontiguousarray(wout_l.reshape(KC, 128, D // OC, OC).transpose(2, 1, 0, 3))
    return {"wZG": wZG, "bZG": bias, "wBr": wBr, "wOut": wOut,
            "lng": np.ascontiguousarray(lng_l[None, :]), "lnb": np.ascontiguousarray(lnb_l[None, :])}


def host_C_inputs(x_l, o_all, cw, TPC):
    NTOK = x_l.shape[0]
    o_full = np.stack([np.asarray(o) for o in o_all], axis=1)
    o_full = o_full.transpose(0, 1, 3, 2, 4).reshape(3, 8 * 65, NTOK)
    maps = []
    for c in range(NCORES):
        sl = slice(c * TPC, (c + 1) * TPC)
        m = dict(cw)
        m["xT"] = np.ascontiguousarray(x_l[sl].T)
        m["xtm"] = np.ascontiguousarray(x_l[sl])
        m["oT3"] = np.ascontiguousarray(o_full[:, :, sl])
        maps.append(m)
    return maps


_CACHE = {}


def _prog(key, fn):
    if key not in _CACHE:
        _CACHE[key] = fn()
    return _CACHE[key]


def kernel(x, w_in, b_in, swa_sinks, w_branch_fox, w_branch_swa, w_branch_moba, w_out, ln_gain, ln_bias):
    x = np.asarray(x, np.float32)
    NBATCH, S, _ = x.shape
    NTOK = NBATCH * S
    TPC = NTOK // NCORES
    cores = list(range(NCORES))
    xl = np.ascontiguousarray(x.reshape(NTOK, D))
    w_in, b_in = np.asarray(w_in, np.float32), np.asarray(b_in, np.float32)
    wbrs = [np.asarray(w, np.float32) for w in (w_branch_fox, w_branch_swa, w_branch_moba)]
    w_out, ln_gain, ln_bias = (np.asarray(a, np.float32) for a in (w_out, ln_gain, ln_bias))
    swa_sinks = np.asarray(swa_sinks, np.float32)
    for l in range(DEPTH):
        ncA = _prog(("A", TPC), lambda: build_A(TPC))
        rA = run_bass_kernel_spmd(ncA, host_A_inputs(xl, w_in[l], b_in[l], TPC), core_ids=cores).results
        qk_full = np.concatenate([np.asarray(r["qkT"]) for r in rA], axis=1)
        f_full = np.concatenate([np.asarray(r["fT"]) for r in rA], axis=1)
        v_full = np.concatenate([np.asarray(r["vtm"]) for r in rA], axis=0)
        del rA
        ncB = _prog(("B", S, NBATCH), lambda: build_B(S, NBATCH))
        rB = run_bass_kernel_spmd(ncB, host_B_inputs(qk_full, f_full, v_full, swa_sinks[l], S, NBATCH),
                                  core_ids=cores).results
        o_all = [np.asarray(r["oT"]) for r in rB]
        del rB, qk_full, f_full, v_full
        ncC = _prog(("C", TPC), lambda: build_C(TPC))
        cw = host_C_weights(w_in[l], b_in[l], [w[l] for w in wbrs], w_out[l], ln_gain[l], ln_bias[l])
        rC = run_bass_kernel_spmd(ncC, host_C_inputs(xl, o_all, cw, TPC), core_ids=cores).results
        xl = np.concatenate([np.asarray(r["y"]) for r in rC], axis=0)
        del rC, o_all
    return xl.reshape(NBATCH, S, D).astype(np.float32)
```

```python
import contextlib
import numpy as np
import ml_dtypes
import concourse.bass as bass
import concourse.mybir as mybir
from concourse.bass_utils import run_bass_kernel_spmd

F32 = mybir.dt.float32
BF16 = mybir.dt.bfloat16
AF = mybir.ActivationFunctionType
ALU = mybir.AluOpType
AX = mybir.AxisListType

NCORES = 8
D = 2048
KC = D // 128
HD = 64
NH = 8
DEPTH = 2
ALPHA = (2.0 * DEPTH) ** 0.25
LN_EPS = 1e-5
NEGBIG = -30000.0
SEM_ROLL = 30000

SEG = {}
_o = 0
for _n, _s in (("fox_q", 512), ("fox_k", 512), ("fox_v", 512), ("fox_z", 512), ("fox_f", 8),
               ("swa_q", 512), ("swa_k", 128), ("swa_v", 128), ("swa_z", 512),
               ("moba_q", 512), ("moba_k", 512), ("moba_v", 512), ("moba_z", 512),
               ("gate_fox", D), ("gate_swa", D), ("gate_moba", D)):
    SEG[_n] = (_o, _s)
    _o += _s
N_IN = _o


class Buf:
    __slots__ = ("name", "w", "r")

    def __init__(self, name=""):
        self.name = name
        self.w = None
        self.r = {}


class DSem:
    def __init__(self, prog, name):
        self.h = prog.ctx.enter_context(prog.nc.semaphore(name))
        self.count = 0
        self.key = ("d", name)


class Prog:
    def __init__(self, nc, ctx, same_engine_sync=True):
        self.nc = nc
        self.ctx = ctx
        self.same_sync = same_engine_sync
        self.eng = {}
        self.semh = {}
        for nm, h in (("pe", nc.tensor), ("act", nc.scalar), ("dve", nc.vector),
                      ("pool", nc.gpsimd), ("sp", nc.sync)):
            self.eng[nm] = {"h": h, "name": nm, "gen": 0, "count": 0, "waited": {}}
            self._newsem(self.eng[nm])
        self.dsems = []
        self.n_inst = 0

    def _newsem(self, e):
        e["gen"] += 1
        e["count"] = 0
        e["key"] = ("e", e["name"], e["gen"])
        e["sem"] = self.ctx.enter_context(self.nc.semaphore(f"s_{e['name']}{e['gen']}"))
        self.semh[e["key"]] = e["sem"]

    def dsem(self, name=None):
        d = DSem(self, name or f"dq{len(self.dsems)}")
        self.dsems.append(d)
        self.semh[d.key] = d.h
        return d

    def _wait(self, e, key, val):
        if val <= e["waited"].get(key, 0):
            return
        e["waited"][key] = val
        e["h"].wait_ge(self.semh[key], val)
        self.n_inst += 1

    def _deps(self, e, reads, writes, same_ok, skip=None):
        k = e["key"]
        for b in reads:
            if b.w is not None and not (same_ok and b.w[0] == k) and b.w[0] != skip:
                self._wait(e, *b.w)
        for b in writes:
            if b.w is not None and not (same_ok and b.w[0] == k) and b.w[0] != skip:
                self._wait(e, *b.w)
            for rk, rv in b.r.items():
                if not (same_ok and rk == k) and rk != skip:
                    self._wait(e, rk, rv)

    @staticmethod
    def _mark(tok, reads, writes):
        for b in reads:
            if b.r.get(tok[0], 0) < tok[1]:
                b.r[tok[0]] = tok[1]
        for b in writes:
            b.w = tok
            b.r = {}

    def op(self, eng, fn, reads=(), writes=(), same_ok=None):
        e = self.eng[eng]
        if same_ok is None:
            same_ok = (eng == "pe") or not self.same_sync
        if e["count"] >= SEM_ROLL:
            self._newsem(e)
        self._deps(e, reads, writes, same_ok)
        ins = fn()
        if isinstance(ins, (list, tuple)):
            self.n_inst += len(ins)
            ins = ins[-1]
        else:
            self.n_inst += 1
        e["count"] += 1
        ins.then_inc(e["sem"], 1)
        tok = (e["key"], e["count"])
        self._mark(tok, reads, writes)
        return tok

    def dma(self, q, out, in_, dsem, reads=(), writes=(), **kw):
        e = self.eng[q]
        self._deps(e, reads, writes, False, skip=dsem.key)
        ins = e["h"].dma_start(out=out, in_=in_, **kw)
        dsem.count += 16
        ins.then_inc(dsem.h, 16)
        self.n_inst += 1
        tok = (dsem.key, dsem.count)
        self._mark(tok, reads, writes)
        return tok

    def finish(self, eng="sp"):
        e = self.eng[eng]
        for d in self.dsems:
            if d.count:
                self._wait(e, d.key, d.count)


def _sb(nc, ctx, name, shape, dt):
    return ctx.enter_context(nc.sbuf_tensor("sb_" + name, list(shape), dt))


def _ps(nc, ctx, name, shape, dt=F32):
    return ctx.enter_context(nc.psum_tensor("pp_" + name, list(shape), dt))


FM_SEGS = ("fox_q", "fox_k", "swa_q", "moba_q", "moba_k", "swa_k")
TM_SEGS = ("fox_v", "swa_v", "moba_v")
QK_ROW = {}
_o = 0
for _n in FM_SEGS:
    QK_ROW[_n] = _o
    _o += SEG[_n][1]
NQK = _o
V_COL = {}
_o = 0
for _n in TM_SEGS:
    V_COL[_n] = _o
    _o += SEG[_n][1]
NTM = _o
NA = NQK + 8 + NTM


def a_col_order():
    cols = []
    for n in FM_SEGS:
        cols += list(range(SEG[n][0], SEG[n][0] + SEG[n][1]))
    cols += list(range(SEG["fox_f"][0], SEG["fox_f"][0] + 8))
    for n in TM_SEGS:
        cols += list(range(SEG[n][0], SEG[n][0] + SEG[n][1]))
    return np.array(cols)


def build_A(TPC):
    nc = bass.Bass("TRN2", target_bir_lowering=False)
    xT = nc.dram_tensor("xT", [D, TPC], F32, kind="ExternalInput").ap()
    wA = nc.dram_tensor("wA", [D, NA], F32, kind="ExternalInput").ap()
    nblk = (NQK + 127) // 128 + 1
    bfm = nc.dram_tensor("bfm", [128, nblk], F32, kind="ExternalInput").ap()
    btm = nc.dram_tensor("btm", [1, NTM], F32, kind="ExternalInput").ap()
    qkT = nc.dram_tensor("qkT", [NQK, TPC], BF16, kind="ExternalOutput").ap()
    fT = nc.dram_tensor("fT", [8, TPC], F32, kind="ExternalOutput").ap()
    vtm = nc.dram_tensor("vtm", [TPC, NTM], BF16, kind="ExternalOutput").ap()
    HT = min(2048, TPC)
    groups = []
    c = 0
    while c < NQK:
        n = min(512, NQK - c)
        groups.append((c, n, "fm"))
        c += n
    groups.append((NQK, 8, "f"))
    c = NQK + 8
    while c < NA:
        n = min(512, NA - c)
        groups.append((c, n, "tm"))
        c += n
    with contextlib.ExitStack() as ctx:
        P = Prog(nc, ctx)
        xh = _sb(nc, ctx, "xh", [128, KC, HT], BF16)
        b_xh = Buf("xh")
        d_xh = P.dsem("d_xh")
        wb = [_sb(nc, ctx, f"wb{i}", [128, KC, 512], BF16) for i in range(2)]
        b_wb = [Buf(f"wb{i}") for i in range(2)]
        d_wb = [P.dsem(f"d_wb{i}") for i in range(2)]
        bfm_sb = _sb(nc, ctx, "bfm_sb", [128, nblk], F32)
        btm_sb = _sb(nc, ctx, "btm_sb", [128, NTM], F32)
        b_bias = Buf("bias")
        d_bias = P.dsem("d_bias")
        NST = 4
        st = [_sb(nc, ctx, f"st{i}", [128, 512], BF16) for i in range(NST)]
        stf = [_sb(nc, ctx, f"stf{i}", [8, 512], F32) for i in range(2)]
        b_st = [Buf(f"st{i}") for i in range(NST)]
        b_stf = [Buf(f"stf{i}") for i in range(2)]
        d_st = [P.dsem(f"d_st{i}") for i in range(NST)]
        d_stf = [P.dsem(f"d_stf{i}") for i in range(2)]
        NPS = 4
        ps = [_ps(nc, ctx, f"ps{i}", [128, 512]) for i in range(NPS)]
        b_ps = [Buf(f"ps{i}") for i in range(NPS)]
        P.dma("sp", bfm_sb[:], bfm, d_bias, writes=[b_bias])
        P.dma("sp", btm_sb[:], btm.partition_broadcast(128), d_bias, writes=[b_bias])
        ips = 0
        ist = 0
        istf = 0
        gi = 0
        for half in range(TPC // HT):
            t0 = half * HT
            for kc in range(KC):
                P.dma("pool", xh[:, kc, :], xT[kc * 128:(kc + 1) * 128, t0:t0 + HT], d_xh, writes=[b_xh])
            for (c0, ncols, kind) in groups:
                w_i = gi % 2
                gi += 1
                P.dma("pool", wb[w_i][:, :, 0:ncols],
                      wA[:, c0:c0 + ncols].rearrange("(kc p) n -> p kc n", p=128),
                      d_wb[w_i], writes=[b_wb[w_i]])
                for tt in range(HT // 512):
                    tok0 = t0 + tt * 512
                    xs = slice(tt * 512, (tt + 1) * 512)
                    if kind in ("fm", "f"):
                        for b0 in range(0, ncols, 128):
                            bw = min(128, ncols - b0)
                            pb = ips % NPS
                            ips += 1
                            blk = (c0 + b0) // 128

                            def mm(pb=pb, bw=bw, b0=b0, w_i=w_i, xs=xs):
                                return [nc.tensor.matmul(ps[pb][0:bw, :], wb[w_i][:, kc, b0:b0 + bw],
                                                         xh[:, kc, xs], start=(kc == 0), stop=(kc == KC - 1))
                                        for kc in range(KC)]
                            P.op("pe", mm, reads=[b_wb[w_i], b_xh], writes=[b_ps[pb]])
                            if kind == "fm":
                                si = ist % NST
                                ist += 1
                                P.op("act", lambda pb=pb, bw=bw, si=si, blk=blk: nc.scalar.activation(
                                    st[si][0:bw, :], ps[pb][0:bw, :], AF.Identity, bias=bfm_sb[0:bw, blk:blk + 1]),
                                    reads=[b_ps[pb], b_bias], writes=[b_st[si]])
                                P.dma("sp", qkT[c0 + b0:c0 + b0 + bw, tok0:tok0 + 512], st[si][0:bw, :], d_st[si],
                                      reads=[b_st[si]])
                            else:
                                si = istf % 2
                                istf += 1
                                P.op("act", lambda pb=pb, si=si, blk=blk: nc.scalar.activation(
                                    stf[si][:, :], ps[pb][0:8, :], AF.Identity, bias=bfm_sb[0:8, blk:blk + 1]),
                                    reads=[b_ps[pb], b_bias], writes=[b_stf[si]])
                                P.dma("sp", fT[:, tok0:tok0 + 512], stf[si][:, :], d_stf[si], reads=[b_stf[si]])
                    else:
                        vc0 = c0 - NQK - 8
                        for sub in range(4):
                            pb = ips % NPS
                            ips += 1
                            ts_ = slice(tt * 512 + sub * 128, tt * 512 + (sub + 1) * 128)

                            def mm(pb=pb, w_i=w_i, ts_=ts_, ncols=ncols):
                                return [nc.tensor.matmul(ps[pb][:, 0:ncols], xh[:, kc, ts_], wb[w_i][:, kc, 0:ncols],
                                                         start=(kc == 0), stop=(kc == KC - 1)) for kc in range(KC)]
                            P.op("pe", mm, reads=[b_wb[w_i], b_xh], writes=[b_ps[pb]])
                            si = ist % NST
                            ist += 1
                            P.op("dve", lambda pb=pb, si=si, ncols=ncols, vc0=vc0: nc.vector.tensor_tensor(
                                st[si][:, 0:ncols], ps[pb][:, 0:ncols], btm_sb[:, vc0:vc0 + ncols], ALU.add),
                                reads=[b_ps[pb], b_bias], writes=[b_st[si]])
                            P.dma("sp", vtm[tok0 + sub * 128:tok0 + (sub + 1) * 128, vc0:vc0 + ncols],
                                  st[si][:, 0:ncols], d_st[si], reads=[b_st[si]])
        P.finish("sp")
        print("build_A n_inst", P.n_inst)
    return nc


def host_A_inputs(x_l, w_in_l, b_in_l, TPC):
    order = a_col_order()
    wA = np.ascontiguousarray(w_in_l[:, order])
    bA = b_in_l[order]
    nblk = (NQK + 127) // 128 + 1
    bfm = np.zeros((128, nblk), np.float32)
    for blk in range(NQK // 128):
        bfm[:, blk] = bA[blk * 128:(blk + 1) * 128]
    bfm[0:8, NQK // 128] = bA[NQK:NQK + 8]
    btm = np.ascontiguousarray(bA[NQK + 8:][None, :])
    maps = []
    for c in range(NCORES):
        xs = x_l[c * TPC:(c + 1) * TPC]
        maps.append({"xT": np.ascontiguousarray(xs.T), "wA": wA, "bfm": bfm, "btm": btm})
    return maps


PSBIG = 240000.0


def b_consts(S):
    NKC = S // 128
    bf = ml_dtypes.bfloat16
    k_ = np.arange(128)[:, None]
    q_ = np.arange(512)[None, :]
    cmask = np.zeros((128, 4, 512), np.float32)
    for j in range(4):
        cmask[:, j, :] = np.where(q_ >= 128 * j + k_, 0.0, -PSBIG)
    q1 = np.arange(128)[None, :]
    smask = np.zeros((128, 2, 128), np.float32)
    smask[:, 0, :] = np.where(q1 >= k_, 0.0, -PSBIG)
    smask[:, 1, :] = np.where(q1 < k_, 0.0, -PSBIG)
    onehot = (np.arange(S)[None, :] // 256 == np.arange(64)[:, None]).astype(np.float32)
    tri = (np.arange(128)[:, None] < np.arange(128)[None, :]).astype(np.float32)
    shared = {"cmask": cmask.astype(bf), "smask": smask.astype(bf), "onehot": onehot.astype(bf),
              "identb": np.eye(128, dtype=np.float32).astype(bf), "identf": np.eye(128, dtype=np.float32),
              "tri": tri}
    per_core = []
    for c in range(NCORES):
        slope = 2.0 ** (-(c + 1))
        pos = np.arange(S, dtype=np.float64)
        mkb = (slope * pos).reshape(NKC, 128).T.astype(np.float32)
        aq = (-8.0 * slope * pos)
        maq = (aq - PSBIG).reshape(NKC, 128).T.astype(np.float32)
        skb = np.stack([slope * np.arange(128), slope * np.arange(128) - 128.0 * slope], axis=1).astype(np.float32)
        sqrow = (-8.0 * slope * (np.arange(S) % 128)).astype(np.float32)[None, :].astype(bf)
        per_core.append({"mkb": mkb, "maq": maq, "skb": skb, "sqrow": sqrow})
    return shared, per_core


def build_B(S, NBATCH=2, do=("swa", "fox", "moba")):
    NKC = S // 128
    NQB = S // 512
    NB = S // 256
    nc = bass.Bass("TRN2", target_bir_lowering=False)

    def din(name, shape, dt):
        return nc.dram_tensor(name, list(shape), dt, kind="ExternalInput").ap()
    qT = {k: din(k + "_q", [NBATCH, 64, S], BF16) for k in ("fox", "swa", "moba")}
    kT = {k: din(k + "_k", [NBATCH, 64, S], BF16) for k in ("fox", "swa", "moba")}
    vv = {k: din(k + "_v", [NBATCH, S, 64], BF16) for k in ("fox", "swa", "moba")}
    ff = din("fox_f", [NBATCH, S], F32)
    sink = din("sink", [1, 1], F32)
    cmask_d = din("cmask", [128, 4, 512], BF16)
    smask_d = din("smask", [128, 2, 128], BF16)
    onehot_d = din("onehot", [64, S], BF16)
    identb_d = din("identb", [128, 128], BF16)
    identf_d = din("identf", [128, 128], F32)
    tri_d = din("tri", [128, 128], F32)
    mkb_d = din("mkb", [128, NKC], F32)
    maq_d = din("maq", [128, NKC], F32)
    skb_d = din("skb", [128, 2], F32)
    sqrow_d = din("sqrow", [1, S], BF16)
    oT = nc.dram_tensor("oT", [3, NBATCH, 65, S], F32, kind="ExternalOutput").ap()
    scr = nc.dram_tensor("scr_sig", [NBATCH, S], BF16).ap()
    KIDX = {"fox": 0, "swa": 1, "moba": 2}

    with contextlib.ExitStack() as ctx:
        P = Prog(nc, ctx)
        sb = lambda n, s, d: _sb(nc, ctx, n, s, d)
        KT = [sb(f"KT{i}", [128, S], BF16) for i in range(2)]
        QT = [sb(f"QT{i}", [128, S], BF16) for i in range(2)]
        VA = [sb(f"VA{i}", [128, NKC, 65], BF16) for i in range(2)]
        tab = [sb(f"tab{i}", [128, NKC], F32) for i in range(2)]
        b_KT = [Buf() for _ in range(2)]
        b_QTq = [Buf() for _ in range(2)]
        b_QTa = [Buf() for _ in range(2)]
        b_VA = [Buf() for _ in range(2)]
        b_tab = [Buf() for _ in range(2)]
        d_KT = [P.dsem(f"d_KT{i}") for i in range(2)]
        d_QT = [P.dsem(f"d_QT{i}") for i in range(2)]
        d_QTa = [P.dsem(f"d_QTa{i}") for i in range(2)]
        d_VA = [P.dsem(f"d_VA{i}") for i in range(2)]
        cmask = sb("cmask", [128, 4, 512], BF16)
        smask = sb("smask", [128, 2, 128], BF16)
        identb = sb("identb", [128, 128], BF16)
        identf = sb("identf", [128, 128], F32)
        tri = sb("tri", [128, 128], F32)
        onesf = sb("onesf", [128, 128], F32)
        mkb = sb("mkb", [128, NKC], F32)
        maq = sb("maq", [128, NKC], F32)
        skb = sb("skb", [128, 2], F32)
        esink = sb("esink", [128, 1], F32)
        b_const = Buf("const")
        d_const = P.dsem("d_const")
        for dst, src in ((cmask, cmask_d), (smask, smask_d), (identb, identb_d), (identf, identf_d),
                         (tri, tri_d), (mkb, mkb_d), (maq, maq_d), (skb, skb_d)):
            P.dma("sp", dst[:], src, d_const, writes=[b_const])
        P.dma("sp", esink[64:65, :], sink, d_const, writes=[b_const])
        for i in range(2):
            P.dma("sp", KT[i][64:128, :], onehot_d, d_const, writes=[b_const])
        b_ones, b_VA1, b_esink = Buf("ones"), Buf("VA1"), Buf("esink")
        P.op("dve", lambda: nc.vector.memset(onesf[:], 1.0), writes=[b_ones])
        P.op("dve", lambda: [nc.vector.memset(VA[i][:, :, 64:65], 1.0) for i in range(2)], writes=[b_VA1])
        P.op("act", lambda: nc.scalar.activation(esink[64:65, :], esink[64:65, :], AF.Exp),
             reads=[b_const], writes=[b_esink])

        NSB = 4
        psS = [_ps(nc, ctx, f"psS{i}", [128, 512]) for i in range(NSB)]
        b_psS = [Buf(f"psS{i}") for i in range(NSB)]
        psO = [_ps(nc, ctx, f"psO{i}", [128, 512]) for i in range(2)]
        b_psO = [Buf(f"psO{i}") for i in range(2)]
        psM = _ps(nc, ctx, "psM", [128, 512])
        b_psM = Buf("psM")
        psT = _ps(nc, ctx, "psT", [128, 128], BF16)
        b_psT = Buf("psT")
        PT = [sb(f"PT{i}", [128, 512], BF16) for i in range(NSB)]
        b_PT = [Buf(f"PT{i}") for i in range(NSB)]
        NOST = 3
        ost = [sb(f"ost{i}", [65, 512], F32) for i in range(NOST)]
        b_ost = [Buf(f"ost{i}") for i in range(NOST)]
        d_ost = [P.dsem(f"d_ost{i}") for i in range(NOST)]
        fkc = sb("fkc", [128, 128], F32)
        fcs = sb("fcs", [128, 128], F32)
        foff = sb("foff", [128, 1], F32)
        fsig = sb("fsig", [128, 128], BF16)
        b_f = Buf("f")
        d_f = P.dsem("d_f")
        d_scr = P.dsem("d_scr")
        b_scr = Buf("scr")
        km = sb("km", [64, 64], F32)
        kmh = sb("kmh", [64, 64], BF16)
        kmhf = sb("kmhf", [64, 64], F32)
        kml = sb("kml", [64, 64], BF16)
        b_km = Buf("km")
        work = sb("work", [128, 64], F32)
        top8 = sb("top8", [128, 8], F32)
        sel = sb("sel", [128, 64], F32)
        MT = sb("MT", [128, 128], BF16)
        b_work, b_top8, b_sel, b_MT = Buf("work"), Buf("top8"), Buf("sel"), Buf("MT")
        st = {"ips": 0, "io": 0, "iost": 0}

        def pro_loads(kind, b, s):
            ops = []
            ops.append(lambda: P.dma("sp", KT[s][0:64, :], kT[kind][b], d_KT[s], writes=[b_KT[s]]))
            ops.append(lambda: P.dma("sp", QT[s][0:64, :], qT[kind][b], d_QT[s], writes=[b_QTq[s]]))
            vsrc = vv[kind][b].rearrange("(kc p) d -> p kc d", p=128)
            nq = 4 if NKC >= 4 else 1
            for qi in range(nq):
                ks = slice(qi * (NKC // nq), (qi + 1) * (NKC // nq))
                ops.append(lambda ks=ks: P.dma("pool", VA[s][:, ks, 0:64], vsrc[:, ks, :], d_VA[s], writes=[b_VA[s]]))
            return ops

        def pro_fox(b, s):
            dm = pro_loads("fox", b, s)
            dm.append(lambda: P.dma("sp", fkc[0:NKC, :], ff[b].rearrange("(kc p) -> kc p", p=128), d_f, writes=[b_f]))
            cp = []
            cp.append(lambda: P.op("act", lambda: nc.scalar.activation(fkc[0:NKC, :], fkc[0:NKC, :], AF.Exp, scale=-1.0),
                                   reads=[b_f], writes=[b_f]))
            cp.append(lambda: P.op("act", lambda: nc.scalar.activation(fkc[0:NKC, :], fkc[0:NKC, :], AF.Ln, bias=1.0),
                                   reads=[b_f], writes=[b_f]))
            cp.append(lambda: P.op("dve", lambda: nc.vector.tensor_tensor_scan(fcs[0:NKC, :], onesf[0:NKC, :], fkc[0:NKC, :],
                                                                               0.0, ALU.mult, ALU.add),
                                   reads=[b_f, b_ones], writes=[b_f]))
            cp.append(lambda: P.op("pe", lambda: nc.tensor.matmul(psM[0:NKC, 0:1], tri[0:NKC, 0:NKC], fcs[0:NKC, 127:128],
                                                                  start=True, stop=True),
                                   reads=[b_f, b_const], writes=[b_psM]))
            cp.append(lambda: P.op("dve", lambda: nc.vector.tensor_copy(foff[0:NKC, :], psM[0:NKC, 0:1]),
                                   reads=[b_psM], writes=[b_f]))
            cp.append(lambda: P.op("dve", lambda: nc.vector.tensor_scalar(fcs[0:NKC, :], fcs[0:NKC, :], foff[0:NKC, 0:1], None,
                                                                          ALU.add), reads=[b_f], writes=[b_f]))
            cp.append(lambda: P.op("dve", lambda: nc.vector.tensor_scalar(fsig[0:NKC, :], fcs[0:NKC, :], -8.0, None, ALU.mult),
                                   reads=[b_f], writes=[b_f]))
            cp.append(lambda: P.dma("sp", scr[b].rearrange("(kc p) -> kc p", p=128), fsig[0:NKC, :], d_scr,
                                    reads=[b_f], writes=[b_scr]))
            cp.append(lambda: P.dma("sp", QT[s][64:128, :], scr[b].partition_broadcast(64), d_QTa[s],
                                    reads=[b_scr], writes=[b_QTa[s]]))
            cp.append(lambda: P.op("pe", lambda: nc.tensor.transpose(psM[:, 0:NKC], fcs[0:NKC, :], identf[0:NKC, 0:NKC]),
                                   reads=[b_f, b_const], writes=[b_psM]))
            cp.append(lambda: P.op("dve", lambda: nc.vector.tensor_copy(tab[s][:, :], psM[:, 0:NKC]),
                                   reads=[b_psM], writes=[b_tab[s]]))
            return dm, cp

        def pro_swa(b, s):
            dm = pro_loads("swa", b, s)
            dm.append(lambda: P.dma("sp", QT[s][64:128, :], sqrow_d[0].partition_broadcast(64), d_QTa[s],
                                    writes=[b_QTa[s]]))
            return dm, []

        def pro_moba(b, s):
            dm = pro_loads("moba", b, s)
            cp = []
            cp.append(lambda: P.op("dve", lambda: nc.vector.tensor_reduce(
                km[:, 0:NB], KT[s][0:64, :].rearrange("d (n k) -> d n k", k=256), AX.X, ALU.add),
                reads=[b_KT[s]], writes=[b_km]))
            cp.append(lambda: P.op("dve", lambda: nc.vector.tensor_scalar(km[:, 0:NB], km[:, 0:NB], 1.0 / 256.0, None, ALU.mult),
                                   reads=[b_km], writes=[b_km]))
            cp.append(lambda: P.op("dve", lambda: nc.vector.tensor_copy(kmh[:, 0:NB], km[:, 0:NB]), reads=[b_km], writes=[b_km]))
            cp.append(lambda: P.op("dve", lambda: nc.vector.tensor_copy(kmhf[:, 0:NB], kmh[:, 0:NB]), reads=[b_km], writes=[b_km]))
            cp.append(lambda: P.op("dve", lambda: nc.vector.tensor_tensor(kml[:, 0:NB], km[:, 0:NB], kmhf[:, 0:NB], ALU.subtract),
                                   reads=[b_km], writes=[b_km]))
            cp.append(lambda: P.op("dve", lambda: nc.vector.memset(work[:, :], -1e30), writes=[b_work]))
            cp.append(lambda: P.op("dve", lambda: nc.vector.memset(sel[:, :], 0.0), writes=[b_sel]))
            cp.append(lambda: P.op("dve", lambda: nc.vector.memset(MT[:, :], 0.0), writes=[b_MT]))
            cp.append(lambda: P.op("dve", lambda: nc.vector.tensor_copy(tab[s][:, :], mkb[:, :]),
                                   reads=[b_const], writes=[b_tab[s]]))
            for t in range(NKC):
                nb = t // 2
                qs = slice(t * 128, (t + 1) * 128)
                if nb > 0:
                    cp.append(lambda qs=qs: P.op("pe", lambda: [
                        nc.tensor.matmul(psM[:, 0:NB], QT[s][0:64, qs], kmh[:, 0:NB], start=True, stop=False),
                        nc.tensor.matmul(psM[:, 0:NB], QT[s][0:64, qs], kml[:, 0:NB], start=False, stop=True)],
                        reads=[b_QTq[s], b_km], writes=[b_psM]))
                    cp.append(lambda nb=nb: P.op("dve", lambda: nc.vector.tensor_copy(work[:, 0:nb], psM[:, 0:nb]),
                                                 reads=[b_psM], writes=[b_work]))
                    cp.append(lambda: P.op("dve", lambda: nc.vector.max(top8[:, :], work[:, 0:max(NB, 8)]),
                                           reads=[b_work], writes=[b_top8]))
                    cp.append(lambda nb=nb: P.op("dve", lambda: nc.vector.tensor_scalar(
                        sel[:, 0:nb], work[:, 0:nb], top8[:, 2:3], None, ALU.is_ge),
                        reads=[b_work, b_top8], writes=[b_sel]))
                if t % 2 == 0:
                    cp.append(lambda nb=nb: P.op("dve", lambda: nc.vector.memset(sel[:, nb:nb + 1], 1.0), writes=[b_sel]))
                cp.append(lambda t=t: P.op("dve", lambda: nc.vector.tensor_scalar(
                    MT[:, 64:128], sel[:, :], PSBIG, maq[:, t:t + 1], ALU.mult, ALU.add),
                    reads=[b_sel, b_const], writes=[b_MT]))
                cp.append(lambda: P.op("pe", lambda: nc.tensor.transpose(psT[:, :], MT[:, :], identb[:, :]),
                                       reads=[b_MT, b_const], writes=[b_psT]))
                cp.append(lambda qs=qs: P.op("dve", lambda: nc.vector.tensor_copy(QT[s][64:128, qs], psT[64:128, :]),
                                             reads=[b_psT], writes=[b_QTa[s]]))
            return dm, cp

        PRO = {"fox": pro_fox, "swa": pro_swa, "moba": pro_moba}

        class Feeder:
            def __init__(self, dm, cp, delay, rate):
                self.dm, self.cp, self.delay, self.rate, self.n = list(dm), list(cp), delay, rate, 0

            def tick(self):
                if self.n == 0:
                    for f in self.dm:
                        f()
                    self.dm = []
                self.n += 1
                if self.n > self.delay:
                    for _ in range(self.rate):
                        if self.cp:
                            self.cp.pop(0)()

            def flush(self):
                for f in self.dm:
                    f()
                for f in self.cp:
                    f()
                self.dm, self.cp = [], []

        def epilogue(kidx, b, ob, q0, nq, add_sink):
            oi = st["iost"] % NOST
            st["iost"] += 1
            P.op("dve", lambda: nc.vector.tensor_copy(ost[oi][:, 0:nq], psO[ob][0:65, 0:nq]),
                 reads=[b_psO[ob]], writes=[b_ost[oi]])
            if add_sink:
                P.op("dve", lambda: nc.vector.tensor_scalar(ost[oi][64:65, 0:nq], ost[oi][64:65, 0:nq],
                                                            esink[64:65, 0:1], None, ALU.add),
                     reads=[b_ost[oi], b_esink], writes=[b_ost[oi]])
            P.dma("sp", oT[kidx, b, :, q0:q0 + nq], ost[oi][:, 0:nq], d_ost[oi], reads=[b_ost[oi]])

        LOOK = 2

        def dense_pass(kidx, b, s, feeder):
            tiles = [(qb, kc) for qb in range(NQB) for kc in range(4 * (qb + 1))]
            pend = []

            def do_pv():
                pqb, pkc, psi = pend.pop(0)
                ob = pqb % 2
                last = pkc == 4 * (pqb + 1) - 1
                P.op("pe", lambda: nc.tensor.matmul(psO[ob][0:65, :], VA[s][:, pkc, :], PT[psi][:, :],
                                                    start=(pkc == 0), stop=last),
                     reads=[b_VA[s], b_VA1, b_PT[psi]], writes=[b_psO[ob]])
                if last:
                    epilogue(kidx, b, ob, pqb * 512, 512, False)
            for (qb, kc) in tiles:
                si = st["ips"] % NSB
                st["ips"] += 1
                diag = kc >= 4 * qb
                j = kc - 4 * qb

                def qk(si=si, qb=qb, kc=kc, diag=diag, j=j):
                    r = [nc.tensor.matmul(psS[si][:, :], KT[s][:, kc * 128:(kc + 1) * 128],
                                          QT[s][:, qb * 512:(qb + 1) * 512], start=True, stop=not diag)]
                    if diag:
                        r.append(nc.tensor.matmul(psS[si][:, :], identb[:, :], cmask[:, j, :],
                                                  start=False, stop=True))
                    return r
                P.op("pe", qk, reads=[b_KT[s], b_QTq[s], b_QTa[s], b_const], writes=[b_psS[si]])
                P.op("act", lambda si=si, kc=kc: nc.scalar.activation(PT[si][:, :], psS[si][:, :], AF.Exp,
                                                                      bias=tab[s][:, kc:kc + 1], scale=0.125),
                     reads=[b_psS[si], b_tab[s]], writes=[b_PT[si]])
                pend.append((qb, kc, si))
                if len(pend) > LOOK:
                    do_pv()
                if feeder is not None:
                    feeder.tick()
            while pend:
                do_pv()

        def swa_pass(b, s, feeder):
            pend = []

            def do_pv():
                g, sa, sp_ = pend.pop(0)
                ob = st["io"] % 2
                st["io"] += 1

                def pv():
                    r = []
                    for jj in range(4):
                        blk = 4 * g + jj
                        cs = slice(jj * 128, (jj + 1) * 128)
                        if blk > 0:
                            r.append(nc.tensor.matmul(psO[ob][0:65, cs], VA[s][:, blk - 1, :], PT[sp_][:, cs],
                                                      start=True, stop=False))
                        r.append(nc.tensor.matmul(psO[ob][0:65, cs], VA[s][:, blk, :], PT[sa][:, cs],
                                                  start=(blk == 0), stop=True))
                    return r
                P.op("pe", pv, reads=[b_VA[s], b_VA1, b_PT[sa], b_PT[sp_]], writes=[b_psO[ob]])
                epilogue(1, b, ob, g * 512, 512, True)
            for g in range(NQB):
                sa = st["ips"] % NSB
                st["ips"] += 1
                sp_ = st["ips"] % NSB
                st["ips"] += 1

                def qk_own(sa=sa, g=g):
                    r = []
                    for jj in range(4):
                        blk = 4 * g + jj
                        cs = slice(jj * 128, (jj + 1) * 128)
                        r.append(nc.tensor.matmul(psS[sa][:, cs], KT[s][:, blk * 128:(blk + 1) * 128],
                                                  QT[s][:, blk * 128:(blk + 1) * 128], start=True, stop=False))
                        r.append(nc.tensor.matmul(psS[sa][:, cs], identb[:, :], smask[:, 0, :], start=False, stop=True))
                    return r

                def qk_prev(sp_=sp_, g=g):
                    r = []
                    for jj in range(4):
                        blk = 4 * g + jj
                        cs = slice(jj * 128, (jj + 1) * 128)
                        if blk > 0:
                            r.append(nc.tensor.matmul(psS[sp_][:, cs], KT[s][:, (blk - 1) * 128:blk * 128],
                                                      QT[s][:, blk * 128:(blk + 1) * 128], start=True, stop=False))
                        r.append(nc.tensor.matmul(psS[sp_][:, cs], identb[:, :], smask[:, 1, :],
                                                  start=(blk == 0), stop=True))
                    return r
                P.op("pe", qk_own, reads=[b_KT[s], b_QTq[s], b_QTa[s], b_const], writes=[b_psS[sa]])
                P.op("pe", qk_prev, reads=[b_KT[s], b_QTq[s], b_QTa[s], b_const], writes=[b_psS[sp_]])
                P.op("act", lambda sa=sa: nc.scalar.activation(PT[sa][:, :], psS[sa][:, :], AF.Exp,
                                                               bias=skb[:, 0:1], scale=0.125),
                     reads=[b_psS[sa], b_const], writes=[b_PT[sa]])
                P.op("act", lambda sp_=sp_: nc.scalar.activation(PT[sp_][:, :], psS[sp_][:, :], AF.Exp,
                                                                 bias=skb[:, 1:2], scale=0.125),
                     reads=[b_psS[sp_], b_const], writes=[b_PT[sp_]])
                pend.append((g, sa, sp_))
                if len(pend) > 1:
                    do_pv()
                if feeder is not None:
                    feeder.tick()
            while pend:
                do_pv()

        passes = [(k, b) for b in range(NBATCH) for k in do]
        dm0, cp0 = PRO[passes[0][0]](passes[0][1], 0)
        Feeder(dm0, cp0, 0, 1).flush()
        for p, (kind, b) in enumerate(passes):
            s = p % 2
            feeder = None
            if p + 1 < len(passes):
                nk, nb_ = passes[p + 1]
                dm, cp = PRO[nk](nb_, (p + 1) % 2)
                if kind == "swa":
                    feeder = Feeder(dm, cp, 8, 2)
                else:
                    feeder = Feeder(dm, cp, 64, 1)
            if kind == "swa":
                swa_pass(b, s, feeder)
            else:
                dense_pass(KIDX[kind], b, s, feeder)
            if feeder is not None:
                feeder.flush()
        P.finish("sp")
        print("build_B n_inst", P.n_inst)
    return nc


def host_B_inputs(qk_full, f_full, v_full, sinks_l, S, NBATCH=2):
    shared, per_core = b_consts(S)
    maps = []

    def qk(name, h):
        r0 = QK_ROW[name] + 64 * h
        return np.ascontiguousarray(qk_full[r0:r0 + 64].reshape(64, NBATCH, S).transpose(1, 0, 2))

    def vh(name, h):
        c0 = V_COL[name] + 64 * h
        return np.ascontiguousarray(v_full[:, c0:c0 + 64].reshape(NBATCH, S, 64))
    for c in range(NCORES):
        m = dict(shared)
        m.update(per_core[c])
        m["fox_q"], m["fox_k"], m["fox_v"] = qk("fox_q", c), qk("fox_k", c), vh("fox_v", c)
        m["moba_q"], m["moba_k"], m["moba_v"] = qk("moba_q", c), qk("moba_k", c), vh("moba_v", c)
        m["swa_q"], m["swa_k"], m["swa_v"] = qk("swa_q", c), qk("swa_k", c // 4), vh("swa_v", c // 4)
        m["fox_f"] = np.ascontiguousarray(f_full[c].reshape(NBATCH, S))
        m["sink"] = np.ascontiguousarray(sinks_l[c].reshape(1, 1)).astype(np.float32)
        maps.append(m)
    return maps


OC = 256
BR = ("fox", "swa", "moba")


def build_C(TPC):
    nc = bass.Bass("TRN2", target_bir_lowering=False)

    def din(name, shape, dt=F32):
        return nc.dram_tensor(name, list(shape), dt, kind="ExternalInput").ap()
    xT = din("xT", [D, TPC])
    xtm = din("xtm", [TPC, D])
    oT3 = din("oT3", [3, 8 * 65, TPC])
    wZG = din("wZG", [60, 128, KC, 128])
    bZG = din("bZG", [128, 60])
    wBr = din("wBr", [48, 128, 4, 128])
    wOut = din("wOut", [D // OC, 128, KC, OC])
    lng = din("lng", [1, D])
    lnb = din("lnb", [1, D])
    y = nc.dram_tensor("y", [TPC, D], F32, kind="ExternalOutput").ap()
    TT = min(1024, TPC)
    rdn = nc.dram_tensor("rdn_scr", [TPC // TT, 24, TT], F32).ap()
    NS = TT // 512
    HS = 512 // 128
    with contextlib.ExitStack() as ctx:
        P = Prog(nc, ctx)
        sb = lambda n, s, d: _sb(nc, ctx, n, s, d)
        xTt = sb("xTt", [128, KC, TT], BF16)
        ozT = sb("ozT", [128, 12, TT], BF16)
        yT = sb("yT", [128, KC, TT], BF16)
        b_xTt, b_yT = Buf("xTt"), Buf("yT")
        b_ozT = [Buf(f"ozT{i}") for i in range(3)]
        d_xTt = P.dsem("d_xTt")
        NW = 4
        wblk = [sb(f"wblk{i}", [128, KC, 128], BF16) for i in range(NW)]
        b_wblk = [Buf(f"wblk{i}") for i in range(NW)]
        d_wblk = [P.dsem(f"d_wblk{i}") for i in range(NW)]
        wbr = [sb(f"wbr{i}", [128, 4, 128], BF16) for i in range(3)]
        b_wbr = [Buf(f"wbr{i}") for i in range(3)]
        d_wbr = [P.dsem(f"d_wbr{i}") for i in range(3)]
        wo = [sb(f"wo{i}", [128, KC, OC], BF16) for i in range(2)]
        b_wo = [Buf(f"wo{i}") for i in range(2)]
        d_wo = [P.dsem(f"d_wo{i}") for i in range(2)]
        bzg = sb("bzg", [128, 60], F32)
        gbc = sb("gbc", [128, D], F32)
        bbc = sb("bbc", [128, D], F32)
        b_const = Buf("const")
        d_const = P.dsem("d_const")
        P.dma("sp", bzg[:], bZG, d_const, writes=[b_const])
        P.dma("sp", gbc[:], lng[0].partition_broadcast(128), d_const, writes=[b_const])
        P.dma("sp", bbc[:], lnb[0].partition_broadcast(128), d_const, writes=[b_const])
        ot = [sb(f"ot{i}", [128, 512], F32) for i in range(2)]
        rd = [sb(f"rd{i}", [128, 512], F32) for i in range(2)]
        dn = sb("dn", [24, TT], F32)
        b_dn = Buf("dn")
        d_dn = P.dsem("d_dn")
        NTILE = TPC // TT
        b_rdn = [Buf(f"rdn{i}") for i in range(NTILE)]
        d_rdn = P.dsem("d_rdn")
        den_view = oT3.rearrange("b (h r) t -> b h r t", r=65)
        b_ot = [Buf(f"ot{i}") for i in range(2)]
        d_ot = [P.dsem(f"d_ot{i}") for i in range(2)]
        sz = [sb(f"sz{i}", [128, 512], F32) for i in range(2)]
        b_sz = [Buf(f"sz{i}") for i in range(2)]
        sg = [sb(f"sg{i}", [128, 512], F32) for i in range(2)]
        b_sg = [Buf(f"sg{i}") for i in range(2)]
        yacc = [sb(f"yacc{i}", [128, 512], F32) for i in range(NS)]
        b_yacc = [Buf(f"yacc{i}") for i in range(NS)]
        tmp = sb("tmp", [128, 512], F32)
        b_tmp = Buf("tmp")
        r = [sb(f"r{i}", [128, D], F32) for i in range(HS)]
        b_r = [Buf(f"r{i}") for i in range(HS)]
        d_rl = [P.dsem(f"d_rl{i}") for i in range(HS)]
        d_rs = [P.dsem(f"d_rs{i}") for i in range(HS)]
        nst = D // 512
        stats = sb("stats", [128, nst, 6], F32)
        mv = sb("mv", [128, 2], F32)
        rstd = sb("rstd", [128, 1], F32)
        b_stats = Buf("stats")
        NPS = 6
        ps = [_ps(nc, ctx, f"ps{i}", [128, 512]) for i in range(NPS)]
        b_ps = [Buf(f"ps{i}") for i in range(NPS)]
        c = {"ps": 0, "w": 0, "ot": 0, "sz": 0, "sg": 0, "wo": 0}

        def nxt(k, n):
            v = c[k] % n
            c[k] += 1
            return v

        def proj16(pb, wi, s):
            return [nc.tensor.matmul(ps[pb][:, :], wblk[wi][:, kc, :], xTt[:, kc, s * 512:(s + 1) * 512],
                                     start=(kc == 0), stop=(kc == KC - 1)) for kc in range(KC)]

        def load_xT(tile):
            tk = tile * TT
            for kc in range(KC):
                P.dma("pool", xTt[:, kc, :], xT[kc * 128:(kc + 1) * 128, tk:tk + TT], d_xTt, writes=[b_xTt])

        def prep_den(tile):
            tk = tile * TT
            for bi in range(3):
                P.dma("sp", dn[bi * 8:(bi + 1) * 8, :], den_view[bi, :, 64, tk:tk + TT], d_dn, writes=[b_dn])
            P.op("dve", lambda: nc.vector.reciprocal(dn[:, :], dn[:, :]), reads=[b_dn], writes=[b_dn])
            P.dma("sp", rdn[tile], dn[:, :], d_rdn, reads=[b_dn], writes=[b_rdn[tile]])

        load_xT(0)
        prep_den(0)
        for tile in range(TPC // TT):
            tok0 = tile * TT
            for br in range(3):
                for wc in range(4):
                    zi = br * 4 + wc
                    wi = nxt("w", NW)
                    P.dma("pool", wblk[wi][:], wZG[zi], d_wblk[wi], writes=[b_wblk[wi]])
                    for s in range(NS):
                        pb = nxt("ps", NPS)
                        P.op("pe", lambda pb=pb, wi=wi, s=s: proj16(pb, wi, s),
                             reads=[b_wblk[wi], b_xTt], writes=[b_ps[pb]])
                        oi = nxt("ot", 2)
                        tsl = slice(tok0 + s * 512, tok0 + (s + 1) * 512)
                        for hh in range(2):
                            r0 = (2 * wc + hh) * 65
                            P.dma("sp", ot[oi][hh * 64:(hh + 1) * 64, :], oT3[br, r0:r0 + 64, tsl], d_ot[oi],
                                  writes=[b_ot[oi]])
                            P.dma("sp", rd[oi][hh * 64:(hh + 1) * 64, :],
                                  rdn[tile, br * 8 + 2 * wc + hh, s * 512:(s + 1) * 512].partition_broadcast(64),
                                  d_ot[oi], reads=[b_rdn[tile]], writes=[b_ot[oi]])
                        P.op("dve", lambda oi=oi: nc.vector.tensor_tensor(ot[oi][:], ot[oi][:], rd[oi][:], ALU.mult),
                             reads=[b_ot[oi]], writes=[b_ot[oi]])
                        zi_ = nxt("sz", 2)
                        P.op("act", lambda pb=pb, zi_=zi_, zi=zi: nc.scalar.activation(
                            sz[zi_][:], ps[pb][:], AF.Silu, bias=bzg[:, zi:zi + 1]),
                            reads=[b_ps[pb], b_const], writes=[b_sz[zi_]])
                        P.op("dve", lambda zi_=zi_, oi=oi, zi=zi, s=s: nc.vector.tensor_tensor(
                            ozT[:, zi, s * 512:(s + 1) * 512], sz[zi_][:], ot[oi][:], ALU.mult),
                            reads=[b_sz[zi_], b_ot[oi]], writes=[b_ozT[br]])
            for dc in range(KC):
                for br in range(3):
                    gi = 12 + dc * 3 + br
                    wi = nxt("w", NW)
                    P.dma("pool", wblk[wi][:], wZG[gi], d_wblk[wi], writes=[b_wblk[wi]])
                    P.dma("pool", wbr[br][:], wBr[dc * 3 + br], d_wbr[br], writes=[b_wbr[br]])
                    for s in range(NS):
                        pg = nxt("ps", NPS)
                        P.op("pe", lambda pg=pg, wi=wi, s=s: proj16(pg, wi, s),
                             reads=[b_wblk[wi], b_xTt], writes=[b_ps[pg]])
                        gi_ = nxt("sg", 2)
                        P.op("act", lambda pg=pg, gi_=gi_, gi=gi: nc.scalar.activation(
                            sg[gi_][:], ps[pg][:], AF.Sigmoid, bias=bzg[:, gi:gi + 1]),
                            reads=[b_ps[pg], b_const], writes=[b_sg[gi_]])
                        pbr = nxt("ps", NPS)
                        P.op("pe", lambda pbr=pbr, br=br, s=s: [
                            nc.tensor.matmul(ps[pbr][:, :], wbr[br][:, wc, :], ozT[:, br * 4 + wc, s * 512:(s + 1) * 512],
                                             start=(wc == 0), stop=(wc == 3)) for wc in range(4)],
                            reads=[b_wbr[br], b_ozT[br]], writes=[b_ps[pbr]])
                        ysl = yT[:, dc, s * 512:(s + 1) * 512]
                        if br == 0:
                            P.op("dve", lambda pbr=pbr, gi_=gi_, s=s: nc.vector.tensor_tensor(
                                yacc[s][:], ps[pbr][:], sg[gi_][:], ALU.mult),
                                reads=[b_ps[pbr], b_sg[gi_]], writes=[b_yacc[s]])
                        else:
                            P.op("dve", lambda pbr=pbr, gi_=gi_: nc.vector.tensor_tensor(
                                tmp[:], ps[pbr][:], sg[gi_][:], ALU.mult),
                                reads=[b_ps[pbr], b_sg[gi_]], writes=[b_tmp])
                            if br == 1:
                                P.op("dve", lambda s=s: nc.vector.tensor_tensor(yacc[s][:], yacc[s][:], tmp[:], ALU.add),
                                     reads=[b_yacc[s], b_tmp], writes=[b_yacc[s]])
                            else:
                                P.op("dve", lambda s=s, ysl=ysl: nc.vector.tensor_tensor(ysl, yacc[s][:], tmp[:], ALU.add),
                                     reads=[b_yacc[s], b_tmp], writes=[b_yT])
            for hf in range(NS):
                for sub in range(HS):
                    t0 = tok0 + hf * 512 + sub * 128
                    P.dma("sp", r[sub][:], xtm[t0:t0 + 128, :], d_rl[sub], writes=[b_r[sub]])
                for ob in range(D // OC):
                    wi = nxt("wo", 2)
                    P.dma("pool", wo[wi][:], wOut[ob], d_wo[wi], writes=[b_wo[wi]])
                    for sub in range(HS):
                        pb = nxt("ps", NPS)
                        tsl = slice(hf * 512 + sub * 128, hf * 512 + (sub + 1) * 128)
                        P.op("pe", lambda pb=pb, wi=wi, tsl=tsl: [
                            nc.tensor.matmul(ps[pb][:, 0:OC], yT[:, kc, tsl], wo[wi][:, kc, :],
                                             start=(kc == 0), stop=(kc == KC - 1)) for kc in range(KC)],
                            reads=[b_yT, b_wo[wi]], writes=[b_ps[pb]])
                        P.op("dve", lambda pb=pb, sub=sub, ob=ob: nc.vector.scalar_tensor_tensor(
                            r[sub][:, ob * OC:(ob + 1) * OC], r[sub][:, ob * OC:(ob + 1) * OC], float(ALPHA),
                            ps[pb][:, 0:OC], ALU.mult, ALU.add),
                            reads=[b_ps[pb], b_r[sub]], writes=[b_r[sub]])
                if hf == NS - 1 and tile + 1 < TPC // TT:
                    load_xT(tile + 1)
                    prep_den(tile + 1)
                for sub in range(HS):
                    t0 = tok0 + hf * 512 + sub * 128
                    P.op("dve", lambda sub=sub: [nc.vector.bn_stats(stats[:, i, :], r[sub][:, i * 512:(i + 1) * 512])
                                                 for i in range(nst)],
                         reads=[b_r[sub]], writes=[b_stats])
                    P.op("dve", lambda: nc.vector.bn_aggr(mv[:, :], stats[:].rearrange("p a b -> p (a b)")),
                         reads=[b_stats], writes=[b_stats])
                    P.op("act", lambda: nc.scalar.activation(rstd[:, :], mv[:, 1:2], AF.Sqrt, bias=float(LN_EPS)),
                         reads=[b_stats], writes=[b_stats])
                    P.op("dve", lambda: nc.vector.reciprocal(rstd[:, :], rstd[:, :]), reads=[b_stats], writes=[b_stats])
                    P.op("dve", lambda sub=sub: nc.vector.scalar_tensor_tensor(r[sub][:], r[sub][:], mv[:, 0:1], gbc[:],
                                                                               ALU.subtract, ALU.mult),
                         reads=[b_r[sub], b_stats, b_const], writes=[b_r[sub]])
                    P.op("dve", lambda sub=sub: nc.vector.scalar_tensor_tensor(r[sub][:], r[sub][:], rstd[:, 0:1], bbc[:],
                                                                               ALU.mult, ALU.add),
                         reads=[b_r[sub], b_stats, b_const], writes=[b_r[sub]])
                    P.dma("sp", y[t0:t0 + 128, :], r[sub][:], d_rs[sub], reads=[b_r[sub]])
        P.finish("sp")
        print("build_C n_inst", P.n_inst)
    return nc


def host_C_weights(w_in_l, b_in_l, wbr_l, wout_l, lng_l, lnb_l):
    blocks = []
    bias = np.zeros((128, 60), np.float32)
    bi = 0
    for br in BR:
        c0 = SEG[br + "_z"][0]
        for wc in range(4):
            blocks.append((c0 + wc * 128))
    for dc in range(KC):
        for br in BR:
            blocks.append(SEG["gate_" + br][0] + dc * 128)
    wZG = np.empty((60, 128, KC, 128), np.float32)
    for bi, c0 in enumerate(blocks):
        wZG[bi] = w_in_l[:, c0:c0 + 128].reshape(KC, 128, 128).transpose(1, 0, 2)
        bias[:, bi] = b_in_l[c0:c0 + 128]
    wBr = np.empty((48, 128, 4, 128), np.float32)
    for dc in range(KC):
        for bri in range(3):
            wBr[dc * 3 + bri] = wbr_l[bri][:, dc * 128:(dc + 1) * 128].reshape(4, 128, 128).transpose(1, 0, 2)
    wOut = np.ascontiguousarray(wout_l.reshape(KC, 128, D // OC, OC).transpose(2, 1, 0, 3))
    return {"wZG": wZG, "bZG": bias, "wBr": wBr, "wOut": wOut,
            "lng": np.ascontiguousarray(lng_l[None, :]), "lnb": np.ascontiguousarray(lnb_l[None, :])}


def host_C_inputs(x_l, o_all, cw, TPC):
    NTOK = x_l.shape[0]
    o_full = np.stack([np.asarray(o) for o in o_all], axis=1)
    o_full = o_full.transpose(0, 1, 3, 2, 4).reshape(3, 8 * 65, NTOK)
    maps = []
    for c in range(NCORES):
        sl = slice(c * TPC, (c + 1) * TPC)
        m = dict(cw)
        m["xT"] = np.ascontiguousarray(x_l[sl].T)
        m["xtm"] = np.ascontiguousarray(x_l[sl])
        m["oT3"] = np.ascontiguousarray(o_full[:, :, sl])
        maps.append(m)
    return maps


_CACHE = {}


def _prog(key, fn):
    if key not in _CACHE:
        _CACHE[key] = fn()
    return _CACHE[key]


def kernel(x, w_in, b_in, swa_sinks, w_branch_fox, w_branch_swa, w_branch_moba, w_out, ln_gain, ln_bias):
    x = np.asarray(x, np.float32)
    NBATCH, S, _ = x.shape
    NTOK = NBATCH * S
    TPC = NTOK // NCORES
    cores = list(range(NCORES))
    xl = np.ascontiguousarray(x.reshape(NTOK, D))
    w_in, b_in = np.asarray(w_in, np.float32), np.asarray(b_in, np.float32)
    wbrs = [np.asarray(w, np.float32) for w in (w_branch_fox, w_branch_swa, w_branch_moba)]
    w_out, ln_gain, ln_bias = (np.asarray(a, np.float32) for a in (w_out, ln_gain, ln_bias))
    swa_sinks = np.asarray(swa_sinks, np.float32)
    for l in range(DEPTH):
        ncA = _prog(("A", TPC), lambda: build_A(TPC))
        rA = run_bass_kernel_spmd(ncA, host_A_inputs(xl, w_in[l], b_in[l], TPC), core_ids=cores).results
        qk_full = np.concatenate([np.asarray(r["qkT"]) for r in rA], axis=1)
        f_full = np.concatenate([np.asarray(r["fT"]) for r in rA], axis=1)
        v_full = np.concatenate([np.asarray(r["vtm"]) for r in rA], axis=0)
        del rA
        ncB = _prog(("B", S, NBATCH), lambda: build_B(S, NBATCH))
        rB = run_bass_kernel_spmd(ncB, host_B_inputs(qk_full, f_full, v_full, swa_sinks[l], S, NBATCH),
                                  core_ids=cores).results
        o_all = [np.asarray(r["oT"]) for r in rB]
        del rB, qk_full, f_full, v_full
        ncC = _prog(("C", TPC), lambda: build_C(TPC))
        cw = host_C_weights(w_in[l], b_in[l], [w[l] for w in wbrs], w_out[l], ln_gain[l], ln_bias[l])
        rC = run_bass_kernel_spmd(ncC, host_C_inputs(xl, o_all, cw, TPC), core_ids=cores).results
        xl = np.concatenate([np.asarray(r["y"]) for r in rC], axis=0)
        del rC, o_all
    return xl.reshape(NBATCH, S, D).astype(np.float32)
```
